# Optimizing a Trainium2 kernel written in Bass

```python
import jax, jax.numpy as jnp
from jax import lax
import numpy as np

D_MODEL = 1024
BATCH = 4
SEQ = 8192
DEPTH = 4

GRID_W = 64
CTX_LEN = 256
N_MIXERS = 2
HGRN_HEAD_DIM = 128
HGRN_HEADS = D_MODEL // HGRN_HEAD_DIM
CHUNK = 64
CONV_WIDTH = 3
D_FF = 4 * D_MODEL
N_REC_LAYERS = (DEPTH + N_MIXERS - 1) // N_MIXERS
N_CONV_LAYERS = DEPTH // N_MIXERS
EPS = 1e-6

kernel_name = 'hybrid_hgrn2_shortconv_dit'


def _rmsnorm(x, gain):
    x32 = x.astype(jnp.float32)
    y = x32 * lax.rsqrt(jnp.mean(x32 * x32, axis=-1, keepdims=True) + EPS)
    return (y * gain.astype(jnp.float32)).astype(x.dtype)


def _modulate(h, shift, scale):
    return h * (1 + scale) + shift


def _mlp(h, w1, w2):
    return jnp.square(jax.nn.relu(h @ w1)) @ w2


def _heads(t):
    return t.reshape(*t.shape[:-1], HGRN_HEADS, HGRN_HEAD_DIM).astype(jnp.float32)


def _to_chunks(t):
    b, l, h, e = t.shape
    return t.reshape(b, l // CHUNK, CHUNK, h, e).transpose(1, 0, 3, 2, 4)


def _gla_chunkwise(q, k, v, g, s0):
    bsz, l, h, _ = q.shape
    lower = jnp.tril(jnp.ones((CHUNK, CHUNK), dtype=bool))

    def step(s, blk):
        qb, kb, vb, gb = blk
        cum = jnp.cumsum(gb, axis=2)
        ref = cum[:, :, CHUNK // 2 - 1:CHUNK // 2]
        last = cum[:, :, -1:]
        o_inter = jnp.einsum('bhck,bhkv->bhcv', qb * jnp.exp(cum), s)
        scores = jnp.einsum('bhck,bhsk->bhcs', qb * jnp.exp(cum - ref), kb * jnp.exp(ref - cum))
        scores = jnp.where(lower, scores, 0.0)
        o_intra = jnp.einsum('bhcs,bhsv->bhcv', scores, vb)
        s_new = jnp.exp(last[:, :, 0, :, None]) * s + jnp.einsum(
            'bhsk,bhsv->bhkv', kb * jnp.exp(last - cum), vb)
        return s_new, o_inter + o_intra

    s_fin, o = lax.scan(step, s0, (_to_chunks(q), _to_chunks(k), _to_chunks(v), _to_chunks(g)))
    o = o.transpose(1, 0, 3, 2, 4).reshape(bsz, l, h, v.shape[-1])
    return o, s_fin


def _gla_final_state(k, v, g):
    cum = jnp.cumsum(g, axis=1)
    return jnp.einsum('blhk,blhv->bhkv', k * jnp.exp(cum[:, -1:] - cum), v)


def _flip(t, direction):
    return jnp.flip(t, axis=1) if direction == 1 else t


def _hgrn2_inputs(h, w_in, lb, with_query):
    n_parts = 5 if with_query else 3
    parts = jnp.split(h @ w_in[:, :n_parts * D_MODEL], n_parts, axis=-1)
    v = _heads(parts[2])
    dirs = []
    for d in range(2):
        lb_d = lb[d].reshape(HGRN_HEADS, HGRN_HEAD_DIM)
        f = lb_d + (1.0 - lb_d) * jax.nn.sigmoid(_heads(parts[d]))
        dirs.append((1.0 - f, jnp.log(f)))
    if with_query:
        return v, dirs, jax.nn.silu(_heads(parts[3])), parts[4]
    return v, dirs, None, None


def _hgrn2_readout(o, gate, gnorm, w_out, dtype):
    o = o * lax.rsqrt(jnp.mean(o * o, axis=-1, keepdims=True) + EPS)
    o = o * gnorm.astype(jnp.float32).reshape(HGRN_HEADS, HGRN_HEAD_DIM) * jax.nn.silu(_heads(gate))
    return o.reshape(*o.shape[:-2], D_MODEL).astype(dtype) @ w_out


def _hgrn2_mixer(h, hc, w_in, lb, gnorm, w_out, ctx_out):
    v, dirs, q, gate = _hgrn2_inputs(h, w_in, lb, True)
    vc, dirs_c, qc, gate_c = _hgrn2_inputs(hc, w_in, lb, ctx_out)
    bsz = h.shape[0]
    o_dirs, oc_dirs = [], []
    for d in range(2):
        k, g = dirs[d]
        kc, gc = dirs_c[d]
        if ctx_out:
            s0 = jnp.zeros((bsz, HGRN_HEADS, HGRN_HEAD_DIM, HGRN_HEAD_DIM), jnp.float32)
            oc_d, s_ctx = _gla_chunkwise(_flip(qc, d), _flip(kc, d), _flip(vc, d), _flip(gc, d), s0)
            oc_dirs.append(_flip(oc_d, d))
        else:
            s_ctx = _gla_final_state(_flip(kc, d), _flip(vc, d), _flip(gc, d))
        o_d, _ = _gla_chunkwise(_flip(q, d), _flip(k, d), _flip(v, d), _flip(g, d), s_ctx)
        o_dirs.append(_flip(o_d, d))
    y = _hgrn2_readout(o_dirs[0] + o_dirs[1], gate, gnorm, w_out, h.dtype)
    yc = _hgrn2_readout(oc_dirs[0] + oc_dirs[1], gate_c, gnorm, w_out, hc.dtype) if ctx_out else None
    return y, yc


def _dwconv(z, w, b, axis):
    n = z.shape[axis]
    half = CONV_WIDTH // 2
    pad = [(0, 0)] * z.ndim
    pad[axis] = (half, half)
    zp = jnp.pad(z, pad)
    y = b
    for tap in range(CONV_WIDTH):
        y = y + w[tap] * lax.slice_in_dim(zp, tap, tap + n, axis=axis)
    return y


def _shortconv_mixer(h, w_in, w, b, w_out, axis):
    gate_b, gate_c, xin = jnp.split(h @ w_in, 3, axis=-1)
    return (gate_b * _dwconv(gate_c * xin, w, b, axis)) @ w_out


def setup_inputs(seed: int = 0) -> dict:
    key = jax.random.key(seed)
    ks = jax.random.split(key, 19)
    nrm = jax.random.normal
    f32 = jnp.float32
    d = D_MODEL
    return {
        'x': nrm(ks[0], (BATCH, SEQ, d), f32),
        'c': nrm(ks[1], (BATCH, d), f32),
        'ctx': nrm(ks[2], (BATCH, CTX_LEN, d), f32),
        'c_ctx': nrm(ks[3], (d,), f32),
        'ada_w': nrm(ks[4], (DEPTH, d, 6 * d), f32) * (0.5 * d ** -0.5),
        'ada_b': 0.02 * nrm(ks[5], (DEPTH, 6 * d), f32),
        'norm1': 1.0 + 0.02 * nrm(ks[6], (DEPTH, d), f32),
        'norm2': 1.0 + 0.02 * nrm(ks[7], (DEPTH, d), f32),
        'norm_f': 1.0 + 0.02 * nrm(ks[8], (d,), f32),
        'mlp_w1': nrm(ks[9], (DEPTH, d, D_FF), f32) * d ** -0.5,
        'mlp_w2': nrm(ks[10], (DEPTH, D_FF, d), f32) * D_FF ** -0.5,
        'hgrn_w_in': nrm(ks[11], (N_REC_LAYERS, d, 5 * d), f32) * d ** -0.5,
        'hgrn_lb': nrm(ks[12], (2, N_REC_LAYERS, d), f32),
        'hgrn_gnorm': 1.0 + 0.02 * nrm(ks[13], (N_REC_LAYERS, d), f32),
        'hgrn_w_out': nrm(ks[14], (N_REC_LAYERS, d, d), f32) * d ** -0.5,
        'conv_w_in': nrm(ks[15], (N_CONV_LAYERS, d, 3 * d), f32) * d ** -0.5,
        'conv_w': nrm(ks[16], (N_CONV_LAYERS, CONV_WIDTH, d), f32) * CONV_WIDTH ** -0.5,
        'conv_b': 0.02 * nrm(ks[17], (N_CONV_LAYERS, d), f32),
        'conv_w_out': nrm(ks[18], (N_CONV_LAYERS, d, d), f32) * d ** -0.5,
    }


def reference(x, c, ctx, c_ctx, ada_w, ada_b, norm1, norm2, norm_f, mlp_w1, mlp_w2,
              hgrn_w_in, hgrn_lb, hgrn_gnorm, hgrn_w_out, conv_w_in, conv_w, conv_b, conv_w_out):
    bsz, seq, _ = x.shape
    rows = seq // GRID_W
    lb_p = jax.nn.softmax(hgrn_lb.astype(jnp.float32), axis=1)
    lower_bounds = jnp.cumsum(lb_p, axis=1) - lb_p[:, :1]
    silu_c = jax.nn.silu(c)
    silu_cc = jax.nn.silu(c_ctx)
    last_rec = ((DEPTH - 1) // N_MIXERS) * N_MIXERS
    x_ctx = ctx
    for i in range(DEPTH):
        j = i // N_MIXERS
        recurrent = i % N_MIXERS == 0
        ctx_live = i < last_rec
        sh1, sc1, g1, sh2, sc2, g2 = jnp.split((silu_c @ ada_w[i] + ada_b[i])[:, None, :], 6, axis=-1)
        h = _modulate(_rmsnorm(x, norm1[i]), sh1, sc1)
        if ctx_live or recurrent:
            csh1, csc1, cg1, csh2, csc2, cg2 = jnp.split(silu_cc @ ada_w[i] + ada_b[i], 6)
            hc = _modulate(_rmsnorm(x_ctx, norm1[i]), csh1, csc1)
        if recurrent:
            y, yc = _hgrn2_mixer(h, hc, hgrn_w_in[j], lower_bounds[:, j], hgrn_gnorm[j],
                                 hgrn_w_out[j], ctx_live)
        else:
            axis = 2 if j % 2 == 0 else 1
            y = _shortconv_mixer(h.reshape(bsz, rows, GRID_W, D_MODEL), conv_w_in[j], conv_w[j],
                                 conv_b[j], conv_w_out[j], axis).reshape(bsz, seq, D_MODEL)
            yc = _shortconv_mixer(hc, conv_w_in[j], conv_w[j], conv_b[j], conv_w_out[j], 1) if ctx_live else None
        x = x + g1 * y
        x = x + g2 * _mlp(_modulate(_rmsnorm(x, norm2[i]), sh2, sc2), mlp_w1[i], mlp_w2[i])
        if ctx_live:
            x_ctx = x_ctx + cg1 * yc
            x_ctx = x_ctx + cg2 * _mlp(_modulate(_rmsnorm(x_ctx, norm2[i]), csh2, csc2), mlp_w1[i], mlp_w2[i])
    return _rmsnorm(x, norm_f)
```

```python
from contextlib import ExitStack
import numpy as np
import concourse.bass as bass
import concourse.mybir as mybir
from concourse.bass_utils import run_bass_kernel_spmd

F32 = mybir.dt.float32
BF16 = mybir.dt.bfloat16
I32 = mybir.dt.int32
AF = mybir.ActivationFunctionType
ALU = mybir.AluOpType

D = 1024
NCH = 8
SEQ = 8192
NB = 4
CTX = 256
NL = 4096
NTOK = NL + CTX
T = 512
DFF = 4096
EPS = 1e-6
SAME_SYNC = True
GROUPS = [[0, 1], [2, 3], [4, 5], [6, 7]]
ALL_STEPS = [("mix", 0), ("mlp", 0), ("mix", 1), ("mlp", 1), ("mix", 2), ("mlp", 2), ("mix", 3), ("mlp", 3)]


class Tok:
    __slots__ = ("sid", "sem", "val", "eng")

    def __init__(self, sid, sem, val, eng):
        self.sid, self.sem, self.val, self.eng = sid, sem, val, eng


class Buf:
    def __init__(self, name):
        self.name = name
        self.w = {}
        self.r = {}


def _add(dct, tok):
    o = dct.get(tok.sid)
    if o is None or o.val < tok.val:
        dct[tok.sid] = tok


class Eng:
    def __init__(self, name, eng, sem, sid):
        self.name, self.eng, self.sem, self.sid = name, eng, sem, sid
        self.count = 0
        self.waited = {}

    def wait(self, tok):
        if tok.eng is self and (self.name == "pe" or not SAME_SYNC):
            return
        if self.waited.get(tok.sid, 0) >= tok.val:
            return
        self.eng.wait_ge(tok.sem, tok.val)
        self.waited[tok.sid] = tok.val


class DSem:
    def __init__(self, sem, sid, step=16):
        self.sem, self.sid, self.count, self.step = sem, sid, 0, step


class TB:
    def __init__(self, t, b, ds=None, bs=None):
        self.t, self.b, self.ds, self.bs = t, b, ds, bs


class K:
    def __init__(self, nc, st):
        self.nc = nc
        self.st = st
        self.nsid = 0
        self.pe = self._eng("pe", nc.tensor)
        self.act = self._eng("act", nc.scalar)
        self.dve = self._eng("dve", nc.vector)
        self.pool = self._eng("pool", nc.gpsimd)
        self.sp = self._eng("sp", nc.sync)
        self.engs = [self.pe, self.act, self.dve, self.pool, self.sp]
        self.dsems = []

    def _sem(self, name):
        s = self.st.enter_context(self.nc.semaphore(name))
        self.nsid += 1
        return s, self.nsid

    def _eng(self, name, eng):
        s, sid = self._sem("e_" + name)
        return Eng(name, eng, s, sid)

    def dsem(self, name):
        self.nsid += 1
        s, sid = self._sem(f"d_{name}_u{self.nsid}")
        d = DSem(s, sid)
        self.dsems.append(d)
        return d

    def _deps(self, E, reads, writes, pw):
        for b in reads:
            for t in b.w.values():
                E.wait(t)
        for b in writes:
            for t in b.w.values():
                E.wait(t)
            for t in b.r.values():
                E.wait(t)
        for b in pw:
            for t in b.r.values():
                E.wait(t)

    def _upd(self, tok, reads, writes, pw):
        for b in reads:
            _add(b.r, tok)
        for b in writes:
            b.w = {tok.sid: tok}
            b.r = {}
        for b in pw:
            _add(b.w, tok)

    def op(self, E, fn, reads=(), writes=(), pw=(), inc=True):
        self._deps(E, reads, writes, pw)
        ins = fn(E.eng)
        if inc:
            ins.then_inc(E.sem, 1)
            E.count += 1
            tok = Tok(E.sid, E.sem, E.count, E)
        else:
            tok = Tok(E.sid, E.sem, E.count + 1, E)
        self._upd(tok, reads, writes, pw)
        return tok

    def dma(self, Q, out, in_, ds, reads=(), writes=(), pw=(), **kw):
        self._deps(Q, reads, writes, pw)
        Q.eng.dma_start(out=out, in_=in_, **kw).then_inc(ds.sem, 16)
        ds.count += 1
        tok = Tok(ds.sid, ds.sem, ds.count * 16, None)
        self._upd(tok, reads, writes, pw)
        return tok

    def barrier(self):
        toks = [Tok(e.sid, e.sem, e.count, e) for e in self.engs if e.count > 0]
        toks += [Tok(d.sid, d.sem, d.count * d.step, None) for d in self.dsems if d.count > 0]
        for e in self.engs:
            for t in toks:
                if t.eng is not e:
                    e.wait(t)

    def sb(self, st, name, shape, dtype, dma=False, nbs=0):
        self.nsid += 1
        name = f"{name}_u{self.nsid}"
        t = st.enter_context(self.nc.sbuf_tensor(name, shape, dtype))
        return TB(t, Buf(name), self.dsem(name) if dma else None, [Buf(f"{name}.{i}") for i in range(nbs)])

    def ps(self, st, name, shape=(128, 512), dtype=F32, nbs=0):
        self.nsid += 1
        name = f"{name}_u{self.nsid}"
        t = st.enter_context(self.nc.psum_tensor(name, list(shape), dtype))
        return TB(t, Buf(name), None, [Buf(f"{name}.{i}") for i in range(nbs)])


class Rot:
    def __init__(self, items):
        self.items, self.i = items, 0

    def next(self):
        x = self.items[self.i % len(self.items)]
        self.i += 1
        return x


class Prog:
    def __init__(self, steps, final):
        self.steps = list(steps)
        self.layers = sorted(set(i for (_, i) in self.steps))
        self.final = final
        nc = self.nc = bass.Bass("TRN2", target_bir_lowering=False)
        dt = nc.dram_tensor

        def inp(name, shape):
            return dt(name, list(shape), F32, kind="ExternalInput").ap()

        self.xin = inp("xin", [D, NTOK])
        self.cvec = inp("cvec", [128, NCH, 2])
        self.consts = inp("consts", [128, 128 + 128 + 512 + 512 + 512 + 64 + 2])
        self.ada_w = inp("ada_w", [4, D, 6 * D])
        self.adab = inp("adab", [128, 4, 48])
        self.nrm = inp("nrm", [128, 9, NCH])
        self.mlp_w1 = inp("mlp_w1", [4, D, DFF])
        self.mlp_w2 = inp("mlp_w2", [4, DFF, D])
        self.hgrn_w_in = inp("hgrn_w_in", [2, D, 5 * D])
        self.lbv = inp("lbv", [128, 2, 2, NCH])
        self.gnorm = inp("gnorm", [128, 2, NCH])
        self.hgrn_w_out = inp("hgrn_w_out", [2, D, D])
        self.conv_w_in = inp("conv_w_in", [2, D, 3 * D])
        self.convw = inp("convw", [128, 2, 3, NCH])
        self.convb = inp("convb", [128, 2, NCH])
        self.conv_w_out = inp("conv_w_out", [2, D, D])
        if final:
            self.yout = dt("yout", [D, NL], F32, kind="ExternalOutput").ap()
        else:
            self.yout = dt("yout", [D, NTOK], F32, kind="ExternalOutput").ap()
        self.xs = dt("xs", [D, NTOK], F32).ap()
        self.o1s = dt("o1s", [D, NTOK], F32).ap()
        self.zs = dt("zs", [D, NL + 64], F32).ap()
        self.gbs = dt("gbs", [D, NL], F32).ap()
        self.cc_in = dt("cc_in", [128, 1024], F32)
        self.cc_out = dt("cc_out", [256, 1024], F32)
        self.cch_in = dt("cch_in", [128, 512], F32)
        self.cch_out = dt("cch_out", [256, 512], F32)
        self.wsc = {}
        self.wsb = {}

        def wscratch(key, Kdim, N, NBc):
            self.wsc[key] = dt("ws_" + key, [N // NBc, 128, Kdim // 128, NBc], BF16).ap()
            self.wsb[key] = Buf("ws_" + key)

        for (kind, i) in self.steps:
            j = i // 2
            if kind == "mlp":
                wscratch(f"w1_{i}", D, DFF, 512)
                wscratch(f"w2_{i}", DFF, D, 128)
            elif i % 2 == 0:
                wscratch(f"hin_{j}", D, 5 * D, 512)
                wscratch(f"hout_{j}", D, D, 128)
            else:
                wscratch(f"cin_{j}", D, 3 * D, 512)
                wscratch(f"cout_{j}", D, D, 128)
        self.jobs_lat = [(q * T, T, 0) for q in range(NL // T)]
        self.job_ctx = (NL, CTX, 1)
        self.xsb = {c0: Buf(f"xs{c0}") for (c0, _, _) in self.jobs_lat + [self.job_ctx]}
        self.o1b = {c0: Buf(f"o1{c0}") for (c0, _, _) in self.jobs_lat + [self.job_ctx]}
        self.zsb = Buf("zs")
        self.gbb = Buf("gbs")
        self.ccb_in = Buf("ccin")
        self.ccb_out = Buf("ccout")
        with ExitStack() as st:
            self.k = K(nc, st)
            self.ccsem = self.k.dsem("cc")
            self.ccsem.step = 1
            self.build(st)

    def xs_ap(self, dram, c0, Tn):
        return dram[:, c0:c0 + Tn].rearrange("(c p) t -> p c t", p=128)

    def build(self, st):
        k = self.k
        nc = self.nc
        self.cst = k.sb(st, "cst", [128, 128 + 128 + 512 + 512 + 512 + 64 + 2], F32, dma=True)
        k.dma(k.sp, self.cst.t[:], self.consts[:, :], self.cst.ds, writes=[self.cst.b])
        c = self.cst.t
        self.ident_f = c[:, 0:128]
        self.ones_f = c[:, 128:256]
        self.smask = c[:, 256:768]
        self.sel = c[:, 1856:1858]
        self.identb = k.sb(st, "identb", [128, 128], BF16)
        self.tri = [k.sb(st, f"tri{d}", [128, 512], I32) for d in range(2)]
        k.op(k.dve, lambda e: e.tensor_copy(out=self.identb.t[:], in_=c[:, 0:128]), reads=[self.cst.b], writes=[self.identb.b])
        for d in range(2):
            k.op(k.dve, lambda e, d=d: e.tensor_copy(out=self.tri[d].t[:], in_=c[:, 768 + 512 * d:768 + 512 * (d + 1)]),
                 reads=[self.cst.b], writes=[self.tri[d].b])
        self.epsb = k.sb(st, "epsb", [128, 1], F32)
        k.op(k.dve, lambda e: e.memset(self.epsb.t[:], EPS), writes=[self.epsb.b])
        self.par = k.sb(st, "par", [128, 4 * 48 + 9 * 8 + 32 + 16 + 48 + 16 + 16], F32, dma=True)
        p = self.par.t
        o = 0
        self.adab_s = p[:, o:o + 192].rearrange("p (i c) -> p i c", i=4); o += 192
        self.nrm_s = p[:, o:o + 72].rearrange("p (i c) -> p i c", i=9); o += 72
        self.lbv_s = p[:, o:o + 32].rearrange("p (d j c) -> p d j c", d=2, j=2); o += 32
        self.gn_s = p[:, o:o + 16].rearrange("p (j c) -> p j c", j=2); o += 16
        self.cw_s = p[:, o:o + 48].rearrange("p (j t c) -> p j t c", j=2, t=3); o += 48
        self.cb_s = p[:, o:o + 16].rearrange("p (j c) -> p j c", j=2); o += 16
        self.cv_s = p[:, o:o + 16].rearrange("p (c j) -> p c j", j=2); o += 16
        for dst, src in ((self.adab_s, self.adab), (self.nrm_s, self.nrm), (self.lbv_s, self.lbv), (self.gn_s, self.gnorm),
                         (self.cw_s, self.convw), (self.cb_s, self.convb), (self.cv_s, self.cvec)):
            k.dma(k.sp, dst, src, self.par.ds, pw=[self.par.b])
        self.setup_ada(st)
        self.convert_weights()
        cp = k.dsem("cpin")
        k.dma(k.sp, self.xs[:, :], self.xin[:, :], cp, pw=list(self.xsb.values()))
        k.barrier()
        for (kind, i) in self.steps:
            j = i // 2
            ctx_live = i < 2
            if kind == "mlp":
                jobs = ([self.job_ctx] if ctx_live else []) + self.jobs_lat
                self.mlp_layer(i, jobs)
            elif i % 2 == 0:
                self.hgrn_layer(i, j, ctx_live)
            else:
                self.conv_layer(i, j, ctx_live)
            k.barrier()
        if self.final:
            self.final_norm()
        else:
            cpo = k.dsem("cpout")
            k.dma(k.sp, self.yout[:, :], self.xs[:, :], cpo, reads=list(self.xsb.values()))
        k.barrier()

    def setup_ada(self, st):
        k = self.k
        self.adaT = k.sb(st, "adaT", [128, 4, 48, 2], F32)
        self.A1 = k.sb(st, "A1", [128, 4, NCH, 2], F32)
        self.A2 = k.sb(st, "A2", [128, 4, NCH, 2], F32)
        self.scv = k.sb(st, "scv", [128, NCH, 2], F32)
        k.op(k.act, lambda e: e.activation(out=self.scv.t[:], in_=self.cv_s, func=AF.Silu), reads=[self.par.b], writes=[self.scv.b])
        with ExitStack() as s2:
            wsl = [k.sb(s2, f"adaw{q}", [128, NCH, 512], F32, dma=True) for q in range(2)]
            pa = k.ps(s2, "ps_ada", (128, 512), F32)
            rot = Rot(wsl)
            for i in self.layers:
                for nb in range(12):
                    w = rot.next()
                    k.dma(k.sp, w.t[:], self.ada_w[i, :, nb * 512:(nb + 1) * 512].rearrange("(c p) n -> p c n", p=128), w.ds, writes=[w.b])
                    for cc in range(4):
                        ch = nb * 4 + cc
                        for kc in range(NCH):
                            k.op(k.pe, lambda e, w=w, cc=cc, kc=kc, ch=ch: e.matmul(
                                pa.t[:, ch * 2:ch * 2 + 2], lhsT=w.t[:, kc, cc * 128:(cc + 1) * 128], rhs=self.scv.t[:, kc, :],
                                start=(kc == 0), stop=(kc == NCH - 1)),
                                reads=[w.b, self.scv.b], pw=[pa.b], inc=(kc == NCH - 1))
                k.op(k.dve, lambda e, i=i: e.tensor_tensor(
                    out=self.adaT.t[:, i], in0=pa.t[:, 0:96].rearrange("p (c j) -> p c j", j=2),
                    in1=self.adab_s[:, i].unsqueeze(2).to_broadcast([128, 48, 2]), op=ALU.add),
                    reads=[pa.b, self.par.b], pw=[self.adaT.b])
                for (A, nidx, split) in ((self.A1, i, 1), (self.A2, 4 + i, 4)):
                    k.op(k.dve, lambda e, A=A, split=split, i=i: e.tensor_scalar(
                        out=A.t[:, i], in0=self.adaT.t[:, i, split * 8:(split + 1) * 8, :], scalar1=1.0, scalar2=None, op0=ALU.add),
                        reads=[self.adaT.b], pw=[A.b])
                    k.op(k.dve, lambda e, A=A, nidx=nidx, i=i: e.tensor_tensor(
                        out=A.t[:, i], in0=A.t[:, i], in1=self.nrm_s[:, nidx].unsqueeze(2).to_broadcast([128, NCH, 2]), op=ALU.mult),
                        reads=[self.par.b, A.b], pw=[A.b])
            k.barrier()

    def ada(self, i, split, c, s):
        return self.adaT.t[:, i, split * 8 + c, s:s + 1]

    def convert_weights(self):
        k = self.k
        with ExitStack() as s2:
            stg = [k.sb(s2, f"stg{q}", [128, 5 * D], BF16, dma=True) for q in range(2)]
            rot = Rot(stg)

            def conv(key, src, Kdim, N):
                dst = self.wsc[key]
                nbc = dst.shape[3]
                for kc in range(Kdim // 128):
                    s = rot.next()
                    k.dma(k.pool, s.t[:, 0:N], src[kc * 128:(kc + 1) * 128, :], s.ds, writes=[s.b], max_dma_last_dim=4096)
                    k.dma(k.sp, dst[:, :, kc, :].rearrange("nb p n -> p nb n"), s.t[:, 0:N].rearrange("p (nb n) -> p nb n", n=nbc),
                          s.ds, reads=[s.b], pw=[self.wsb[key]])

            for (kind, i) in self.steps:
                j = i // 2
                if kind == "mlp":
                    conv(f"w1_{i}", self.mlp_w1[i], D, DFF)
                    conv(f"w2_{i}", self.mlp_w2[i], DFF, D)
                elif i % 2 == 0:
                    conv(f"hin_{j}", self.hgrn_w_in[j], D, 5 * D)
                    conv(f"hout_{j}", self.hgrn_w_out[j], D, D)
                else:
                    conv(f"cin_{j}", self.conv_w_in[j], D, 3 * D)
                    conv(f"cout_{j}", self.conv_w_out[j], D, D)
            k.barrier()

    def norm_mod(self, x, Tn, hT, big, rt, rstd, psb, A, i, shsplit, s, nidx_unused=None):
        k = self.k
        k.op(k.act, lambda e: e.activation(out=big.t[:, :, :Tn], in_=x.t[:, :, :Tn], func=AF.Square), reads=[x.b], writes=[big.b])
        for c in range(NCH):
            k.op(k.pe, lambda e, c=c: e.matmul(psb.t[:, :Tn], lhsT=self.ones_f, rhs=big.t[:, c, :Tn], start=(c == 0), stop=(c == NCH - 1)),
                 reads=[big.b, self.cst.b], writes=[psb.b] if c == 0 else [], pw=[psb.b] if c else [], inc=(c == NCH - 1))
        k.op(k.act, lambda e: e.activation(out=rt.t[:, :Tn], in_=psb.t[:, :Tn], func=AF.Sqrt, bias=self.epsb.t[:, 0:1], scale=1.0 / D),
             reads=[psb.b, self.epsb.b], writes=[rt.b])
        k.op(k.dve, lambda e: e.reciprocal(out=rstd.t[:, :Tn], in_=rt.t[:, :Tn]), reads=[rt.b], writes=[rstd.b])
        k.op(k.dve, lambda e: e.tensor_tensor(out=big.t[:, :, :Tn], in0=x.t[:, :, :Tn],
                                              in1=rstd.t[:, :Tn].unsqueeze(1).to_broadcast([128, NCH, Tn]), op=ALU.mult),
             reads=[x.b, rstd.b], writes=[big.b])
        for c in range(NCH):
            k.op(k.pool, lambda e, c=c: e.tensor_scalar(out=hT.t[:, c, :Tn], in0=big.t[:, c, :Tn], scalar1=A.t[:, i, c, s:s + 1],
                                                         scalar2=self.ada(i, shsplit, c, s), op0=ALU.mult, op1=ALU.add),
                 reads=[big.b, A.b, self.adaT.b], writes=[hT.b] if c == 0 else [], pw=[hT.b] if c else [])

    def mlp_layer(self, i, jobs):
        k = self.k
        w1s, w2s = self.wsc[f"w1_{i}"], self.wsc[f"w2_{i}"]
        w1b, w2b = self.wsb[f"w1_{i}"], self.wsb[f"w2_{i}"]
        with ExitStack() as st:
            xt = Rot([k.sb(st, f"m_x{q}", [128, NCH, T], F32, dma=True) for q in range(2)])
            hT = k.sb(st, "m_hT", [128, NCH, T], BF16)
            big = k.sb(st, "m_big", [128, NCH, T], F32)
            rt = k.sb(st, "m_rt", [128, T], F32)
            rstd = k.sb(st, "m_rstd", [128, T], F32)
            hid = k.sb(st, "m_hid", [128, 32, T], BF16, nbs=32)
            rb = Rot([k.sb(st, f"m_r{q}", [128, T], F32) for q in range(3)])
            ws = Rot([k.sb(st, f"m_w{q}", [128, 4096], BF16, dma=True) for q in range(3)])
            ps = Rot([k.ps(st, f"m_ps{q}") for q in range(6)])
            psn = k.ps(st, "m_psn")
            for (c0, Tn, s) in jobs:
                x = xt.next()
                k.dma(k.sp, x.t[:, :, :Tn], self.xs_ap(self.xs, c0, Tn), x.ds, reads=[self.xsb[c0]], writes=[x.b])
                self.norm_mod(x, Tn, hT, big, rt, rstd, psn, self.A2, i, 3, s)
                for nb in range(8):
                    w = ws.next()
                    k.dma(k.sp, w.t[:, :], w1s[nb].rearrange("p k n -> p (k n)"), w.ds, reads=[w1b], writes=[w.b])
                    for nn in range(4):
                        n = nb * 4 + nn
                        p = ps.next()
                        for kc in range(NCH):
                            k.op(k.pe, lambda e, w=w, p=p, kc=kc, nn=nn: e.matmul(
                                p.t[:, :Tn], lhsT=w.t[:, kc * 512 + nn * 128:kc * 512 + (nn + 1) * 128], rhs=hT.t[:, kc, :Tn],
                                start=(kc == 0), stop=(kc == NCH - 1)),
                                reads=[w.b, hT.b], writes=[p.b] if kc == 0 else [], pw=[p.b] if kc else [], inc=(kc == NCH - 1))
                        r = rb.next()
                        k.op(k.act, lambda e, r=r, p=p: e.activation(out=r.t[:, :Tn], in_=p.t[:, :Tn], func=AF.Relu), reads=[p.b], writes=[r.b])
                        k.op(k.pool, lambda e, r=r, n=n: e.tensor_tensor(out=hid.t[:, n, :Tn], in0=r.t[:, :Tn], in1=r.t[:, :Tn], op=ALU.mult),
                             reads=[r.b], writes=[hid.bs[n]])
                for c in range(NCH):
                    w = ws.next()
                    k.dma(k.sp, w.t[:, :], w2s[c].rearrange("p k n -> p (k n)"), w.ds, reads=[w2b], writes=[w.b])
                    p = ps.next()
                    for kc in range(32):
                        k.op(k.pe, lambda e, w=w, p=p, kc=kc: e.matmul(
                            p.t[:, :Tn], lhsT=w.t[:, kc * 128:(kc + 1) * 128], rhs=hid.t[:, kc, :Tn], start=(kc == 0), stop=(kc == 31)),
                            reads=[w.b, hid.bs[kc]], writes=[p.b] if kc == 0 else [], pw=[p.b] if kc else [], inc=(kc == 31))
                    k.op(k.dve, lambda e, p=p, c=c, x=x: e.scalar_tensor_tensor(
                        out=x.t[:, c, :Tn], in0=p.t[:, :Tn], scalar=self.ada(i, 5, c, s), in1=x.t[:, c, :Tn], op0=ALU.mult, op1=ALU.add),
                        reads=[p.b, self.adaT.b, x.b], pw=[x.b])
                k.dma(k.sp, self.xs_ap(self.xs, c0, Tn), x.t[:, :, :Tn], x.ds, reads=[x.b], writes=[self.xsb[c0]])

    def final_norm(self):
        k = self.k
        with ExitStack() as st:
            xt = Rot([k.sb(st, f"f_x{q}", [128, NCH, T], F32, dma=True) for q in range(2)])
            big = Rot([k.sb(st, f"f_big{q}", [128, NCH, T], F32, dma=True) for q in range(2)])
            rt = k.sb(st, "f_rt", [128, T], F32)
            rstd = k.sb(st, "f_rstd", [128, T], F32)
            psn = k.ps(st, "f_psn")
            yb = Buf("yout")
            for (c0, Tn, s) in self.jobs_lat:
                x = xt.next()
                bg = big.next()
                k.dma(k.sp, x.t[:, :, :Tn], self.xs_ap(self.xs, c0, Tn), x.ds, reads=[self.xsb[c0]], writes=[x.b])
                k.op(k.act, lambda e: e.activation(out=bg.t[:, :, :Tn], in_=x.t[:, :, :Tn], func=AF.Square), reads=[x.b], writes=[bg.b])
                for c in range(NCH):
                    k.op(k.pe, lambda e, c=c: e.matmul(psn.t[:, :Tn], lhsT=self.ones_f, rhs=bg.t[:, c, :Tn], start=(c == 0), stop=(c == NCH - 1)),
                         reads=[bg.b, self.cst.b], writes=[psn.b] if c == 0 else [], pw=[psn.b] if c else [], inc=(c == NCH - 1))
                k.op(k.act, lambda e: e.activation(out=rt.t[:, :Tn], in_=psn.t[:, :Tn], func=AF.Sqrt, bias=self.epsb.t[:, 0:1], scale=1.0 / D),
                     reads=[psn.b, self.epsb.b], writes=[rt.b])
                k.op(k.dve, lambda e: e.reciprocal(out=rstd.t[:, :Tn], in_=rt.t[:, :Tn]), reads=[rt.b], writes=[rstd.b])
                k.op(k.dve, lambda e: e.tensor_tensor(out=bg.t[:, :, :Tn], in0=x.t[:, :, :Tn],
                                                      in1=rstd.t[:, :Tn].unsqueeze(1).to_broadcast([128, NCH, Tn]), op=ALU.mult),
                     reads=[x.b, rstd.b], writes=[bg.b])
                k.op(k.pool, lambda e: e.tensor_tensor(out=bg.t[:, :, :Tn], in0=bg.t[:, :, :Tn],
                                                       in1=self.nrm_s[:, 8].unsqueeze(2).to_broadcast([128, NCH, Tn]), op=ALU.mult),
                     reads=[bg.b, self.par.b], writes=[bg.b])
                k.dma(k.sp, self.xs_ap(self.yout, c0, Tn), bg.t[:, :, :Tn], bg.ds, reads=[bg.b], pw=[yb])


    def hgrn_layer(self, i, j, ctx_live):
        k = self.k
        wins, winb = self.wsc[f"hin_{j}"], self.wsb[f"hin_{j}"]
        wos, wob = self.wsc[f"hout_{j}"], self.wsb[f"hout_{j}"]
        with ExitStack() as st:
            x = k.sb(st, "h_x", [128, NCH, T], F32, dma=True)
            hT = k.sb(st, "h_hT", [128, NCH, T], BF16)
            big = k.sb(st, "h_big", [128, NCH, T], F32, dma=True)
            ot = k.sb(st, "h_ot", [128, NCH, T], F32, dma=True)
            rt = k.sb(st, "h_rt", [128, T], F32)
            rstd = k.sb(st, "h_rstd", [128, T], F32)
            tnames = ["sg", "g", "kk", "cum", "c2", "dd", "ll", "e1", "sq"]
            tsets = Rot([{n: k.sb(st, f"h_{n}{q}", [128, T], F32) for n in tnames} for q in range(2)])
            bsets = Rot([{n: k.sb(st, f"h_{n}{q}", [128, T], BF16) for n in ("qd", "kd", "kl")} for q in range(2)])
            qeT = k.sb(st, "h_qeT", [128, NCH, T], BF16, nbs=NCH)
            scT = k.sb(st, "h_scT", [128, NCH, T], BF16, nbs=NCH)
            kltm = k.sb(st, "h_kltm", [128, 4, NCH, 128], BF16, nbs=NCH)
            vtm = k.sb(st, "h_vtm", [128, 4, D], BF16)
            elast = k.sb(st, "h_elast", [128, NCH, 8], F32, nbs=NCH)
            S = k.sb(st, "h_S", [128, NCH, 128], F32, dma=True, nbs=NCH)
            Sb = k.sb(st, "h_Sb", [128, NCH, 128], BF16, nbs=NCH)
            on = k.sb(st, "h_on", [128, NCH, T], BF16, nbs=NCH)
            LB = k.sb(st, "h_LB", [128, 2, NCH], F32)
            OML = k.sb(st, "h_OML", [128, 2, NCH], F32)
            ws = Rot([k.sb(st, f"h_w{q}", [128, 4096], BF16, dma=True) for q in range(3)])
            wo = k.sb(st, "h_wo", [128, NCH, NCH, 128], BF16, dma=True)
            pp = Rot([k.ps(st, f"h_pp{q}") for q in range(2)])
            pT = k.ps(st, "h_pT", (128, 1024), BF16)
            pS = k.ps(st, "h_pS")
            pO = Rot([k.ps(st, f"h_pO{q}") for q in range(2)])
            pP = Rot([k.ps(st, f"h_pP{q}") for q in range(2)])
            k.dma(k.sp, wo.t[:], wos.rearrange("nb p k n -> p nb k n"), wo.ds, reads=[wob], writes=[wo.b])
            if j == 0:
                k.op(k.dve, lambda e: e.memset(LB.t[:], 0.0), writes=[LB.b])
                k.op(k.dve, lambda e: e.memset(OML.t[:], 1.0), writes=[OML.b])
            else:
                k.op(k.dve, lambda e: e.tensor_tensor(out=LB.t[:], in0=self.lbv_s[:, :, 1, :], in1=self.lbv_s[:, :, 0, :], op=ALU.subtract),
                     reads=[self.par.b], writes=[LB.b])
                k.op(k.act, lambda e: e.activation(out=LB.t[:], in_=LB.t[:], func=AF.Sigmoid), reads=[LB.b], writes=[LB.b])
                k.op(k.dve, lambda e: e.tensor_scalar(out=OML.t[:], in0=LB.t[:], scalar1=-1.0, scalar2=1.0, op0=ALU.mult, op1=ALU.add),
                     reads=[LB.b], writes=[OML.b])

            def loadw(nb):
                w = ws.next()
                k.dma(k.sp, w.t[:, :], wins[nb].rearrange("p k n -> p (k n)"), w.ds, reads=[winb], writes=[w.b])
                return w

            def proj_chunk(w, nn, Tn):
                p = pp.next()
                for kc in range(NCH):
                    k.op(k.pe, lambda e: e.matmul(p.t[:, :Tn], lhsT=w.t[:, kc * 512 + nn * 128:kc * 512 + (nn + 1) * 128], rhs=hT.t[:, kc, :Tn],
                                                  start=(kc == 0), stop=(kc == NCH - 1)),
                         reads=[w.b, hT.b], writes=[p.b] if kc == 0 else [], pw=[p.b] if kc else [], inc=(kc == NCH - 1))
                return p

            def bc(ap3, nch):
                return ap3.to_broadcast([128, nch, 64])

            def zero_state():
                k.op(k.dve, lambda e: e.memset(S.t[:], 0.0), writes=S.bs)
                k.op(k.dve, lambda e: e.memset(Sb.t[:], 0.0), writes=Sb.bs)

            for d in range(2):
                Ridx, Lidx = (31, 63) if d == 0 else (32, 0)
                k.op(k.pool, lambda e: e.memset(scT.t[:], 0.0), writes=scT.bs)
                if d == 0:
                    zero_state()
                    jobs = [self.job_ctx] + self.jobs_lat
                else:
                    jobs = self.jobs_lat[::-1] + ([self.job_ctx] if ctx_live else [])
                for (c0, Tn, s) in jobs:
                    nch, nsub = Tn // 64, Tn // 128
                    if d == 1 and s == 1:
                        zero_state()
                    need_o = not (s == 1 and not ctx_live)
                    k.dma(k.sp, x.t[:, :, :Tn], self.xs_ap(self.xs, c0, Tn), x.ds, reads=[self.xsb[c0]], writes=[x.b])
                    self.norm_mod(x, Tn, hT, big, rt, rstd, pp.next(), self.A1, i, 0, s)
                    for hg in range(2):
                        wf = loadw(2 * d + hg)
                        wq = loadw(6 + hg)
                        for hh in range(4):
                            hd = hg * 4 + hh
                            t = tsets.next()
                            b = bsets.next()
                            sg, g, kk, cum, c2, dd, ll, e1, sq = (t[n] for n in tnames)
                            pF = proj_chunk(wf, hh, Tn)
                            k.op(k.act, lambda e: e.activation(out=sg.t[:, :Tn], in_=pF.t[:, :Tn], func=AF.Sigmoid), reads=[pF.b], writes=[sg.b])
                            pQ = proj_chunk(wq, hh, Tn)
                            k.op(k.act, lambda e: e.activation(out=sq.t[:, :Tn], in_=pQ.t[:, :Tn], func=AF.Silu), reads=[pQ.b], writes=[sq.b])
                            k.op(k.dve, lambda e: e.tensor_scalar(out=sg.t[:, :Tn], in0=sg.t[:, :Tn], scalar1=OML.t[:, d, hd:hd + 1], scalar2=LB.t[:, d, hd:hd + 1],
                                                                  op0=ALU.mult, op1=ALU.add), reads=[sg.b, OML.b, LB.b], writes=[sg.b])
                            k.op(k.act, lambda e: e.activation(out=g.t[:, :Tn], in_=sg.t[:, :Tn], func=AF.Ln), reads=[sg.b], writes=[g.b])
                            k.op(k.pool, lambda e: e.tensor_scalar(out=kk.t[:, :Tn], in0=sg.t[:, :Tn], scalar1=-1.0, scalar2=1.0, op0=ALU.mult, op1=ALU.add),
                                 reads=[sg.b], writes=[kk.b])
                            k.op(k.dve, lambda e: e.tensor_tensor_scan(out=cum.t[:, :Tn], data0=self.smask[:, :Tn], data1=g.t[:, :Tn], initial=0.0,
                                                                       op0=ALU.mult, op1=ALU.add), reads=[g.b, self.cst.b], writes=[cum.b])
                            cm = cum
                            if d == 1:
                                cum3 = cum.t[:, :Tn].rearrange("p (n m) -> p n m", m=64)
                                k.op(k.pool, lambda e: e.tensor_tensor(out=sg.t[:, :Tn], in0=g.t[:, :Tn], in1=cum.t[:, :Tn], op=ALU.subtract),
                                     reads=[g.b, cum.b], writes=[sg.b])
                                k.op(k.pool, lambda e: e.tensor_tensor(out=c2.t[:, :Tn].rearrange("p (n m) -> p n m", m=64),
                                                                       in0=sg.t[:, :Tn].rearrange("p (n m) -> p n m", m=64),
                                                                       in1=bc(cum3[:, :, 63:64], nch), op=ALU.add),
                                     reads=[sg.b, cum.b], writes=[c2.b])
                                cm = c2
                            cm3 = cm.t[:, :Tn].rearrange("p (n m) -> p n m", m=64)
                            k.op(k.dve, lambda e: e.tensor_tensor(out=dd.t[:, :Tn].rearrange("p (n m) -> p n m", m=64), in0=cm3,
                                                                  in1=bc(cm3[:, :, Ridx:Ridx + 1], nch), op=ALU.subtract), reads=[cm.b], writes=[dd.b])
                            k.op(k.pool, lambda e: e.tensor_tensor(out=ll.t[:, :Tn].rearrange("p (n m) -> p n m", m=64), in0=cm3,
                                                                   in1=bc(cm3[:, :, Lidx:Lidx + 1], nch), op=ALU.subtract), reads=[cm.b], writes=[ll.b])
                            k.op(k.act, lambda e: e.activation(out=e1.t[:, :Tn], in_=dd.t[:, :Tn], func=AF.Exp), reads=[dd.b], writes=[e1.b])
                            k.op(k.act, lambda e: e.activation(out=dd.t[:, :Tn], in_=dd.t[:, :Tn], func=AF.Exp, scale=-1.0), reads=[dd.b], writes=[dd.b])
                            k.op(k.act, lambda e: e.activation(out=g.t[:, :Tn], in_=cm.t[:, :Tn], func=AF.Exp), reads=[cm.b], writes=[g.b])
                            k.op(k.act, lambda e: e.activation(out=ll.t[:, :Tn], in_=ll.t[:, :Tn], func=AF.Exp, scale=-1.0), reads=[ll.b], writes=[ll.b])
                            k.op(k.act, lambda e: e.activation(out=elast.t[:, hd, :nch].unsqueeze(2), in_=cm3[:, :, Lidx:Lidx + 1], func=AF.Exp),
                                 reads=[cm.b], writes=[elast.bs[hd]])
                            k.op(k.dve, lambda e: e.tensor_tensor(out=b["qd"].t[:, :Tn], in0=sq.t[:, :Tn], in1=e1.t[:, :Tn], op=ALU.mult),
                                 reads=[sq.b, e1.b], writes=[b["qd"].b])
                            k.op(k.pool, lambda e: e.tensor_tensor(out=b["kd"].t[:, :Tn], in0=kk.t[:, :Tn], in1=dd.t[:, :Tn], op=ALU.mult),
                                 reads=[kk.b, dd.b], writes=[b["kd"].b])
                            k.op(k.dve, lambda e: e.tensor_tensor(out=qeT.t[:, hd, :Tn], in0=sq.t[:, :Tn], in1=g.t[:, :Tn], op=ALU.mult),
                                 reads=[sq.b, g.b], writes=[qeT.bs[hd]])
                            k.op(k.pool, lambda e: e.tensor_tensor(out=b["kl"].t[:, :Tn], in0=kk.t[:, :Tn], in1=ll.t[:, :Tn], op=ALU.mult),
                                 reads=[kk.b, ll.b], writes=[b["kl"].b])
                            for sub in range(nsub):
                                k.op(k.pe, lambda e: e.transpose(pT.t[:, sub * 128:(sub + 1) * 128], b["kl"].t[:, sub * 128:(sub + 1) * 128], self.identb.t[:]),
                                     reads=[b["kl"].b, self.identb.b], writes=[pT.b] if sub == 0 else [], pw=[pT.b] if sub else [], inc=(sub == nsub - 1))
                            k.op(k.act, lambda e: e.activation(out=kltm.t[:, :nsub, hd, :], in_=pT.t[:, :nsub * 128].rearrange("p (s n) -> p s n", n=128), func=AF.Copy),
                                 reads=[pT.b], writes=[kltm.bs[hd]])
                            for sub in range(nsub):
                                sl = slice(sub * 128, (sub + 1) * 128)
                                k.op(k.pe, lambda e: e.matmul(pS.t[:, sl], lhsT=b["kd"].t[:, sl], rhs=b["qd"].t[:, sl], start=True, stop=True),
                                     reads=[b["kd"].b, b["qd"].b], writes=[pS.b] if sub == 0 else [], pw=[pS.b] if sub else [], inc=(sub == nsub - 1))
                            k.op(k.dve, lambda e: e.copy_predicated(out=scT.t[:, hd, :Tn], mask=self.tri[d].t[:, :Tn], data=pS.t[:, :Tn]),
                                 reads=[pS.b, self.tri[d].b], pw=[scT.bs[hd]])
                    wv = [loadw(4), loadw(5)]
                    for sub in range(nsub):
                        for half in range(2):
                            p = pp.next()
                            for kc in range(NCH):
                                k.op(k.pe, lambda e: e.matmul(p.t[:, :], lhsT=hT.t[:, kc, sub * 128:(sub + 1) * 128], rhs=wv[half].t[:, kc * 512:(kc + 1) * 512],
                                                              start=(kc == 0), stop=(kc == NCH - 1)),
                                     reads=[wv[half].b, hT.b], writes=[p.b] if kc == 0 else [], pw=[p.b] if kc else [], inc=(kc == NCH - 1))
                            k.op(k.act, lambda e: e.activation(out=vtm.t[:, sub, half * 512:(half + 1) * 512], in_=p.t[:, :], func=AF.Copy),
                                 reads=[p.b], writes=[vtm.b] if (sub == 0 and half == 0) else [], pw=[] if (sub == 0 and half == 0) else [vtm.b])
                    order = list(range(nch)) if d == 0 else list(range(nch))[::-1]
                    for ci in order:
                        sub, hf = divmod(ci, 2)
                        pr = slice(hf * 64, hf * 64 + 64)
                        po = pO.next()
                        for hd in range(NCH):
                            if need_o:
                                k.op(k.pe, lambda e: e.matmul(po.t[:, hd * 64:(hd + 1) * 64], lhsT=Sb.t[:, hd, :], rhs=qeT.t[:, hd, ci * 64:(ci + 1) * 64],
                                                              start=True, stop=False), reads=[Sb.bs[hd], qeT.bs[hd]], pw=[po.b], inc=False)
                                k.op(k.pe, lambda e: e.matmul(po.t[:, hd * 64:(hd + 1) * 64], lhsT=vtm.t[pr, sub, hd * 128:(hd + 1) * 128],
                                                              rhs=scT.t[pr, hd, sub * 128 + hf * 64:sub * 128 + hf * 64 + 64], start=False, stop=True),
                                     reads=[vtm.b, scT.bs[hd]], pw=[po.b], inc=True)
                            if hd % 4 == 0:
                                p4 = pP.next()
                            k.op(k.pe, lambda e: e.matmul(p4.t[:, (hd % 4) * 128:(hd % 4 + 1) * 128], lhsT=kltm.t[pr, sub, hd, :],
                                                          rhs=vtm.t[pr, sub, hd * 128:(hd + 1) * 128], start=True, stop=True),
                                 reads=[kltm.bs[hd], vtm.b], pw=[p4.b], inc=True)
                            k.op(k.dve, lambda e: e.scalar_tensor_tensor(out=S.t[:, hd, :], in0=S.t[:, hd, :], scalar=elast.t[:, hd, ci:ci + 1],
                                                                         in1=p4.t[:, (hd % 4) * 128:(hd % 4 + 1) * 128], op0=ALU.mult, op1=ALU.add),
                                 reads=[S.bs[hd], elast.bs[hd], p4.b], writes=[S.bs[hd]])
                            k.op(k.act, lambda e: e.activation(out=Sb.t[:, hd, :], in_=S.t[:, hd, :], func=AF.Copy), reads=[S.bs[hd]], writes=[Sb.bs[hd]])
                        if need_o:
                            k.op(k.act, lambda e: e.activation(out=ot.t[:, :, ci * 64:(ci + 1) * 64], in_=po.t[:, :].rearrange("p (h m) -> p h m", m=64), func=AF.Copy),
                                 reads=[po.b], pw=[ot.b])
                    if d == 0:
                        if need_o:
                            k.dma(k.sp, self.xs_ap(self.o1s, c0, Tn), ot.t[:, :, :Tn], ot.ds, reads=[ot.b], writes=[self.o1b[c0]])
                    else:
                        k.dma(k.sp, big.t[:, :, :Tn], self.xs_ap(self.o1s, c0, Tn), big.ds, reads=[self.o1b[c0]], writes=[big.b])
                        k.op(k.dve, lambda e: e.tensor_tensor(out=ot.t[:, :, :Tn], in0=ot.t[:, :, :Tn], in1=big.t[:, :, :Tn], op=ALU.add),
                             reads=[ot.b, big.b], writes=[ot.b])
                        k.op(k.act, lambda e: e.activation(out=big.t[:, :, :Tn], in_=ot.t[:, :, :Tn], func=AF.Square), reads=[ot.b], writes=[big.b])
                        for hg in range(2):
                            wg = loadw(8 + hg)
                            for hh in range(4):
                                hd = hg * 4 + hh
                                t = tsets.next()
                                pn = pp.next()
                                k.op(k.pe, lambda e: e.matmul(pn.t[:, :Tn], lhsT=self.ones_f, rhs=big.t[:, hd, :Tn], start=True, stop=True),
                                     reads=[big.b, self.cst.b], writes=[pn.b])
                                k.op(k.act, lambda e: e.activation(out=t["sg"].t[:, :Tn], in_=pn.t[:, :Tn], func=AF.Sqrt, bias=self.epsb.t[:, 0:1], scale=1.0 / 128),
                                     reads=[pn.b, self.epsb.b], writes=[t["sg"].b])
                                k.op(k.dve, lambda e: e.reciprocal(out=t["g"].t[:, :Tn], in_=t["sg"].t[:, :Tn]), reads=[t["sg"].b], writes=[t["g"].b])
                                pg = proj_chunk(wg, hh, Tn)
                                k.op(k.act, lambda e: e.activation(out=t["sq"].t[:, :Tn], in_=pg.t[:, :Tn], func=AF.Silu), reads=[pg.b], writes=[t["sq"].b])
                                k.op(k.dve, lambda e: e.scalar_tensor_tensor(out=t["kk"].t[:, :Tn], in0=ot.t[:, hd, :Tn], scalar=self.gn_s[:, j, hd:hd + 1],
                                                                             in1=t["g"].t[:, :Tn], op0=ALU.mult, op1=ALU.mult),
                                     reads=[ot.b, self.par.b, t["g"].b], writes=[t["kk"].b])
                                k.op(k.pool, lambda e: e.tensor_tensor(out=on.t[:, hd, :Tn], in0=t["kk"].t[:, :Tn], in1=t["sq"].t[:, :Tn], op=ALU.mult),
                                     reads=[t["kk"].b, t["sq"].b], writes=[on.bs[hd]])
                        for c in range(NCH):
                            p = pp.next()
                            for kc in range(NCH):
                                k.op(k.pe, lambda e: e.matmul(p.t[:, :Tn], lhsT=wo.t[:, c, kc, :], rhs=on.t[:, kc, :Tn], start=(kc == 0), stop=(kc == NCH - 1)),
                                     reads=[wo.b, on.bs[kc]], writes=[p.b] if kc == 0 else [], pw=[p.b] if kc else [], inc=(kc == NCH - 1))
                            k.op(k.dve, lambda e: e.scalar_tensor_tensor(out=x.t[:, c, :Tn], in0=p.t[:, :Tn], scalar=self.ada(i, 2, c, s), in1=x.t[:, c, :Tn],
                                                                         op0=ALU.mult, op1=ALU.add),
                                 reads=[p.b, self.adaT.b, x.b], pw=[x.b])
                        k.dma(k.sp, self.xs_ap(self.xs, c0, Tn), x.t[:, :, :Tn], x.ds, reads=[x.b], writes=[self.xsb[c0]])
                if d == 0:
                    k.dma(k.pool, self.cc_in.ap(), S.t[:].rearrange("p h v -> p (h v)"), S.ds, reads=S.bs, writes=[self.ccb_in])
                    self.collective(self.cc_in, self.cc_out)
                    hbv = big.t[:, 0:4, :].rearrange("p (r a) t -> p r (a t)", r=2)
                    k.dma(k.pool, hbv, self.cc_out.ap().rearrange("(r p) n -> p r n", p=128), big.ds, reads=[self.ccb_out], writes=[big.b])
                    k.op(k.dve, lambda e: e.tensor_scalar(out=S.t[:].rearrange("p h v -> p (h v)"), in0=hbv[:, 0, :], scalar1=self.sel[:, 0:1], scalar2=None, op0=ALU.mult),
                         reads=[big.b, self.cst.b], writes=S.bs)
                    k.op(k.dve, lambda e: e.scalar_tensor_tensor(out=S.t[:].rearrange("p h v -> p (h v)"), in0=hbv[:, 1, :], scalar=self.sel[:, 1:2],
                                                                 in1=S.t[:].rearrange("p h v -> p (h v)"), op0=ALU.mult, op1=ALU.add),
                         reads=[big.b, self.cst.b] + S.bs, writes=S.bs)
                    k.op(k.act, lambda e: e.activation(out=Sb.t[:], in_=S.t[:], func=AF.Copy), reads=S.bs, writes=Sb.bs)


    def conv_layer(self, i, j, ctx_live):
        k = self.k
        axis2 = (j % 2 == 0)
        wins, winb = self.wsc[f"cin_{j}"], self.wsb[f"cin_{j}"]
        wos, wob = self.wsc[f"cout_{j}"], self.wsb[f"cout_{j}"]
        cw = lambda tap, c: self.cw_s[:, j, tap, c:c + 1]
        cb = lambda c: self.cb_s[:, j, c:c + 1]
        with ExitStack() as st:
            xt = Rot([k.sb(st, f"c_x{q}", [128, NCH, T], F32, dma=True) for q in range(1)])
            hT = k.sb(st, "c_hT", [128, NCH, T], BF16)
            big = k.sb(st, "c_big", [128, NCH, T], F32)
            rt = k.sb(st, "c_rt", [128, T], F32)
            rstd = k.sb(st, "c_rstd", [128, T], F32)
            zh = Rot([k.sb(st, f"c_zh{q}", [128, NCH, T + 128], F32, dma=True) for q in range(2)])
            acc = k.sb(st, "c_acc", [128, NCH, T], F32, nbs=NCH)
            gbt = Rot([k.sb(st, f"c_gb{q}", [128, NCH, T], F32, dma=True) for q in range(1)])
            u = k.sb(st, "c_u", [128, NCH, T], BF16, nbs=NCH)
            tmp = Rot([k.sb(st, f"c_tmp{q}", [128, T], F32) for q in range(3)])
            ws = Rot([k.sb(st, f"c_w{q}", [128, 4096], BF16, dma=True) for q in range(4)])
            wo = k.sb(st, "c_wo", [128, NCH, NCH, 128], BF16, dma=True)
            ps = Rot([k.ps(st, f"c_ps{q}") for q in range(6)])
            psn = k.ps(st, "c_psn")
            k.dma(k.sp, wo.t[:], wos.rearrange("nb p k n -> p nb k n"), wo.ds, reads=[wob], writes=[wo.b])

            def proj_chunk(w, nn, Tn):
                p = ps.next()
                for kc in range(NCH):
                    k.op(k.pe, lambda e: e.matmul(p.t[:, :Tn], lhsT=w.t[:, kc * 512 + nn * 128:kc * 512 + (nn + 1) * 128], rhs=hT.t[:, kc, :Tn],
                                                  start=(kc == 0), stop=(kc == NCH - 1)),
                         reads=[w.b, hT.b], writes=[p.b] if kc == 0 else [], pw=[p.b] if kc else [], inc=(kc == NCH - 1))
                return p

            def loadw(nb):
                w = ws.next()
                k.dma(k.sp, w.t[:, :], wins[nb].rearrange("p k n -> p (k n)"), w.ds, reads=[winb], writes=[w.b])
                return w

            def compute_z(zt, zoff, Tn):
                for cbk in range(2):
                    wgc = loadw(2 + cbk)
                    wxi = loadw(4 + cbk)
                    for cc in range(4):
                        c = cbk * 4 + cc
                        p1 = proj_chunk(wgc, cc, Tn)
                        tp = tmp.next()
                        k.op(k.act, lambda e: e.activation(out=tp.t[:, :Tn], in_=p1.t[:, :Tn], func=AF.Copy), reads=[p1.b], writes=[tp.b])
                        p2 = proj_chunk(wxi, cc, Tn)
                        k.op(k.dve, lambda e: e.tensor_tensor(out=zt.t[:, c, zoff:zoff + Tn], in0=tp.t[:, :Tn], in1=p2.t[:, :Tn], op=ALU.mult),
                             reads=[tp.b, p2.b], pw=[zt.b])

            def out_stage(x, Tn, s, gb_from_psum):
                for cbk in range(2):
                    wgb = loadw(cbk) if gb_from_psum is None else None
                    for cc in range(4):
                        c = cbk * 4 + cc
                        if gb_from_psum is None:
                            p = proj_chunk(wgb, cc, Tn)
                            k.op(k.dve, lambda e: e.tensor_tensor(out=u.t[:, c, :Tn], in0=acc.t[:, c, :Tn], in1=p.t[:, :Tn], op=ALU.mult),
                                 reads=[acc.bs[c], p.b], writes=[u.bs[c]])
                        else:
                            g = gb_from_psum
                            k.op(k.pool, lambda e: e.tensor_tensor(out=u.t[:, c, :Tn], in0=acc.t[:, c, :Tn], in1=g.t[:, c, :Tn], op=ALU.mult),
                                 reads=[acc.bs[c], g.b], writes=[u.bs[c]])
                for c in range(NCH):
                    p = ps.next()
                    for kc in range(NCH):
                        k.op(k.pe, lambda e: e.matmul(p.t[:, :Tn], lhsT=wo.t[:, c, kc, :], rhs=u.t[:, kc, :Tn], start=(kc == 0), stop=(kc == NCH - 1)),
                             reads=[wo.b, u.bs[kc]], writes=[p.b] if kc == 0 else [], pw=[p.b] if kc else [], inc=(kc == NCH - 1))
                    k.op(k.dve, lambda e: e.scalar_tensor_tensor(out=x.t[:, c, :Tn], in0=p.t[:, :Tn], scalar=self.ada(i, 2, c, s), in1=x.t[:, c, :Tn],
                                                                 op0=ALU.mult, op1=ALU.add),
                         reads=[p.b, self.adaT.b, x.b], pw=[x.b])

            def taps(zt, zoff, Tn, mode):
                for c in range(NCH):
                    k.op(k.pool, lambda e: e.tensor_scalar(out=acc.t[:, c, :Tn], in0=zt.t[:, c, zoff:zoff + Tn], scalar1=cw(1, c), scalar2=cb(c),
                                                           op0=ALU.mult, op1=ALU.add),
                         reads=[zt.b, self.par.b], writes=[acc.bs[c]])
                    if mode == "rows":
                        a3 = acc.t[:, c, :Tn].rearrange("p (r m) -> p r m", m=64)
                        z3 = zt.t[:, c, zoff:zoff + Tn].rearrange("p (r m) -> p r m", m=64)
                        pairs = [(a3[:, :, 1:64], z3[:, :, 0:63], 0), (a3[:, :, 0:63], z3[:, :, 1:64], 2)]
                    elif mode == "seq":
                        pairs = [(acc.t[:, c, 1:Tn], zt.t[:, c, zoff:zoff + Tn - 1], 0), (acc.t[:, c, 0:Tn - 1], zt.t[:, c, zoff + 1:zoff + Tn], 2)]
                    else:
                        pairs = [(acc.t[:, c, :Tn], zt.t[:, c, zoff - 64:zoff - 64 + Tn], 0), (acc.t[:, c, :Tn], zt.t[:, c, zoff + 64:zoff + 64 + Tn], 2)]
                    for (a_ap, z_ap, tap) in pairs:
                        k.op(k.dve, lambda e: e.scalar_tensor_tensor(out=a_ap, in0=z_ap, scalar=cw(tap, c), in1=a_ap, op0=ALU.mult, op1=ALU.add),
                             reads=[zt.b, self.par.b, acc.bs[c]], writes=[acc.bs[c]])

            if axis2:
                jobs = ([self.job_ctx] if ctx_live else []) + self.jobs_lat
                for (c0, Tn, s) in jobs:
                    x = xt.next()
                    k.dma(k.sp, x.t[:, :, :Tn], self.xs_ap(self.xs, c0, Tn), x.ds, reads=[self.xsb[c0]], writes=[x.b])
                    self.norm_mod(x, Tn, hT, big, rt, rstd, psn, self.A1, i, 0, s)
                    zt = zh.next()
                    compute_z(zt, 0, Tn)
                    taps(zt, 0, Tn, "seq" if s == 1 else "rows")
                    out_stage(x, Tn, s, None)
                    k.dma(k.sp, self.xs_ap(self.xs, c0, Tn), x.t[:, :, :Tn], x.ds, reads=[x.b], writes=[self.xsb[c0]])
            else:
                zt_last = None
                for (c0, Tn, s) in self.jobs_lat:
                    x = xt.next()
                    k.dma(k.sp, x.t[:, :, :Tn], self.xs_ap(self.xs, c0, Tn), x.ds, reads=[self.xsb[c0]], writes=[x.b])
                    self.norm_mod(x, Tn, hT, big, rt, rstd, psn, self.A1, i, 0, s)
                    zt = zh.next()
                    compute_z(zt, 0, Tn)
                    k.dma(k.sp, self.xs_ap(self.zs, c0, Tn), zt.t[:, :, 0:Tn], zt.ds, reads=[zt.b], pw=[self.zsb])
                    g = gbt.next()
                    for cbk in range(2):
                        wgb = loadw(cbk)
                        for cc in range(4):
                            c = cbk * 4 + cc
                            p = proj_chunk(wgb, cc, Tn)
                            k.op(k.act, lambda e: e.activation(out=g.t[:, c, :Tn], in_=p.t[:, :Tn], func=AF.Copy), reads=[p.b], pw=[g.b])
                    k.dma(k.sp, self.xs_ap(self.gbs, c0, Tn), g.t[:, :, :Tn], g.ds, reads=[g.b], pw=[self.gbb])
                    zt_last = zt
                hb = k.sb(st, "c_hb", [128, 2, 512], F32, dma=True)
                hs = k.sb(st, "c_hs", [128, NCH, 64], F32, dma=True)
                hr = k.sb(st, "c_hr", [128, NCH, 64], F32, dma=True)
                cci = self.cch_in
                cco = self.cch_out
                k.dma(k.pool, cci.ap().rearrange("p (c m) -> p c m", m=64), zt_last.t[:, :, T - 64:T], hb.ds, reads=[zt_last.b], writes=[self.ccb_in])
                self.collective(cci, cco)
                k.dma(k.pool, hb.t[:], cco.ap().rearrange("(r p) n -> p r n", p=128), hb.ds, reads=[self.ccb_out], writes=[hb.b])
                self.select_partner(hb, hs.t[:].rearrange("p c m -> p (c m)"), hs.b, 512)
                k.op(k.dve, lambda e: e.tensor_copy(out=hr.t[:], in_=hs.t[:, :, ::-1]), reads=[hs.b], writes=[hr.b])
                k.dma(k.sp, self.xs_ap(self.zs, NL, 64), hr.t[:], hr.ds, reads=[hr.b], pw=[self.zsb])
                zbufs = zh.items
                for zb_ in zbufs:
                    k.op(k.pool, lambda e: e.memset(zb_.t[:, :, 0:64], 0.0), writes=[zb_.b])
                zr = Rot(zbufs)
                for (c0, Tn, s) in self.jobs_lat:
                    x = xt.next()
                    k.dma(k.sp, x.t[:, :, :Tn], self.xs_ap(self.xs, c0, Tn), x.ds, reads=[self.xsb[c0]], writes=[x.b])
                    zt = zr.next()
                    if c0 == 0:
                        k.dma(k.sp, zt.t[:, :, 64:Tn + 128], self.xs_ap(self.zs, 0, Tn + 64), zt.ds, reads=[self.zsb], pw=[zt.b])
                    else:
                        k.dma(k.sp, zt.t[:, :, 0:Tn + 128], self.xs_ap(self.zs, c0 - 64, Tn + 128), zt.ds, reads=[self.zsb], writes=[zt.b])
                    g = gbt.next()
                    k.dma(k.sp, g.t[:, :, :Tn], self.xs_ap(self.gbs, c0, Tn), g.ds, reads=[self.gbb], writes=[g.b])
                    taps(zt, 64, Tn, "halo")
                    out_stage(x, Tn, s, g)
                    k.dma(k.sp, self.xs_ap(self.xs, c0, Tn), x.t[:, :, :Tn], x.ds, reads=[x.b], writes=[self.xsb[c0]])

    def collective(self, cin, cout):
        k = self.k
        E = k.pool
        k._deps(E, [self.ccb_in], [self.ccb_out], [])
        E.eng.collective_compute("AllGather", ALU.bypass, replica_groups=GROUPS, ins=[cin.ap().opt()], outs=[cout.ap().opt()]).then_inc(self.ccsem.sem, 1)
        self.ccsem.count += 1
        tok = Tok(self.ccsem.sid, self.ccsem.sem, self.ccsem.count, None)
        k._upd(tok, [self.ccb_in], [self.ccb_out], [])

    def select_partner(self, hb, out_ap, out_buf, n):
        k = self.k
        k.op(k.dve, lambda e: e.tensor_scalar(out=out_ap, in0=hb.t[:, 0, :n], scalar1=self.sel[:, 0:1], scalar2=None, op0=ALU.mult),
             reads=[hb.b, self.cst.b], writes=[out_buf])
        k.op(k.dve, lambda e: e.scalar_tensor_tensor(out=out_ap, in0=hb.t[:, 1, :n], scalar=self.sel[:, 1:2], in1=out_ap, op0=ALU.mult, op1=ALU.add),
             reads=[hb.b, self.cst.b, out_buf], writes=[out_buf])


def _consts():
    c = np.zeros((128, 128 + 128 + 512 + 512 + 512 + 64 + 2), np.float32)
    c[:, 0:128] = np.eye(128, dtype=np.float32)
    c[:, 128:256] = 1.0
    sm = np.ones((128, 512), np.float32)
    sm[:, ::64] = 0.0
    c[:, 256:768] = sm
    s_idx = np.arange(128)[:, None]
    c_idx = np.arange(128)[None, :]
    same = (s_idx // 64) == (c_idx // 64)
    m1 = (same & (c_idx >= s_idx)).astype(np.float32)
    m2 = (same & (c_idx <= s_idx)).astype(np.float32)
    c[:, 768:1280] = np.tile(m1, (1, 4))
    c[:, 1280:1792] = np.tile(m2, (1, 4))
    c[:64, 1792:1856] = np.eye(64, dtype=np.float32)[::-1]
    return c


def _pp(v):
    v = np.asarray(v, np.float32)
    lead = v.shape[:-1]
    return np.ascontiguousarray(np.moveaxis(v.reshape(*lead, NCH, 128), -1, 0))


def make_in_maps(inputs, xstate=None):
    x, c, ctx, c_ctx = inputs["x"], inputs["c"], inputs["ctx"], inputs["c_ctx"]
    consts = _consts()
    maps = []
    for r in range(8):
        b, half = r // 2, r % 2
        m = {}
        if xstate is not None:
            m["xin"] = xstate[r]
        else:
            if half == 0:
                xl = x[b, :NL]
                cl = ctx[b]
            else:
                xl = x[b, NL:][::-1]
                cl = ctx[b][::-1]
            m["xin"] = np.ascontiguousarray(np.concatenate([xl, cl], axis=0).T)
        cv = np.stack([c[b], c_ctx], axis=-1)
        m["cvec"] = np.ascontiguousarray(cv.reshape(NCH, 128, 2).transpose(1, 0, 2))
        cs = consts.copy()
        cs[:, 1856 + (1 - half)] = 1.0
        m["consts"] = cs
        m["ada_w"] = inputs["ada_w"]
        m["adab"] = np.ascontiguousarray(inputs["ada_b"].reshape(4, 48, 128).transpose(2, 0, 1))
        m["nrm"] = np.ascontiguousarray(np.concatenate([_pp(inputs["norm1"]), _pp(inputs["norm2"]), _pp(inputs["norm_f"][None])], axis=1))
        m["mlp_w1"] = inputs["mlp_w1"]
        m["mlp_w2"] = inputs["mlp_w2"]
        hw = inputs["hgrn_w_in"]
        lb = inputs["hgrn_lb"]
        cw = inputs["conv_w"]
        if half == 1:
            hw = np.concatenate([hw[:, :, D:2 * D], hw[:, :, 0:D], hw[:, :, 2 * D:]], axis=2)
            lb = lb[::-1]
            cw = cw[:, ::-1]
        m["hgrn_w_in"] = np.ascontiguousarray(hw)
        m["lbv"] = _pp(lb)
        m["gnorm"] = _pp(inputs["hgrn_gnorm"])
        m["hgrn_w_out"] = inputs["hgrn_w_out"]
        m["conv_w_in"] = inputs["conv_w_in"]
        m["convw"] = _pp(cw)
        m["convb"] = _pp(inputs["conv_b"])
        m["conv_w_out"] = inputs["conv_w_out"]
        maps.append(m)
    return maps


def gather_out(res):
    out = np.empty((NB, SEQ, D), np.float32)
    for r in range(8):
        b, half = r // 2, r % 2
        y = res[r]["yout"].T
        if half == 0:
            out[b, :NL] = y
        else:
            out[b, NL:] = y[::-1]
    return out


def kernel(**inputs):
    inputs = {k: np.asarray(v) for k, v in inputs.items()}
    prog = Prog(ALL_STEPS, True)
    maps = make_in_maps(inputs)
    res = run_bass_kernel_spmd(prog.nc, maps, core_ids=list(range(8)))
    return gather_out(res.results)
```

```python
from contextlib import ExitStack
import numpy as np
import concourse.bass as bass
import concourse.mybir as mybir
from concourse.bass_utils import run_bass_kernel_spmd

F32 = mybir.dt.float32
BF16 = mybir.dt.bfloat16
I32 = mybir.dt.int32
AF = mybir.ActivationFunctionType
ALU = mybir.AluOpType

D = 1024
NCH = 8
SEQ = 8192
NB = 4
CTX = 256
NL = 4096
NTOK = NL + CTX
T = 512
DFF = 4096
EPS = 1e-6
SAME_SYNC = True
GROUPS = [[0, 1], [2, 3], [4, 5], [6, 7]]
ALL_STEPS = [("mix", 0), ("mlp", 0), ("mix", 1), ("mlp", 1), ("mix", 2), ("mlp", 2), ("mix", 3), ("mlp", 3)]


class Tok:
    __slots__ = ("sid", "sem", "val", "eng")

    def __init__(self, sid, sem, val, eng):
        self.sid, self.sem, self.val, self.eng = sid, sem, val, eng


class Buf:
    def __init__(self, name):
        self.name = name
        self.w = {}
        self.r = {}


def _add(dct, tok):
    o = dct.get(tok.sid)
    if o is None or o.val < tok.val:
        dct[tok.sid] = tok


class Eng:
    def __init__(self, name, eng, sem, sid):
        self.name, self.eng, self.sem, self.sid = name, eng, sem, sid
        self.count = 0
        self.waited = {}

    def wait(self, tok):
        if tok.eng is self and (self.name == "pe" or not SAME_SYNC):
            return
        if self.waited.get(tok.sid, 0) >= tok.val:
            return
        self.eng.wait_ge(tok.sem, tok.val)
        self.waited[tok.sid] = tok.val


class DSem:
    def __init__(self, sem, sid, step=16):
        self.sem, self.sid, self.count, self.step = sem, sid, 0, step


class TB:
    def __init__(self, t, b, ds=None, bs=None):
        self.t, self.b, self.ds, self.bs = t, b, ds, bs


class K:
    def __init__(self, nc, st):
        self.nc = nc
        self.st = st
        self.nsid = 0
        self.pe = self._eng("pe", nc.tensor)
        self.act = self._eng("act", nc.scalar)
        self.dve = self._eng("dve", nc.vector)
        self.pool = self._eng("pool", nc.gpsimd)
        self.sp = self._eng("sp", nc.sync)
        self.engs = [self.pe, self.act, self.dve, self.pool, self.sp]
        self.dsems = []

    def _sem(self, name):
        s = self.st.enter_context(self.nc.semaphore(name))
        self.nsid += 1
        return s, self.nsid

    def _eng(self, name, eng):
        s, sid = self._sem("e_" + name)
        return Eng(name, eng, s, sid)

    def dsem(self, name):
        self.nsid += 1
        s, sid = self._sem(f"d_{name}_u{self.nsid}")
        d = DSem(s, sid)
        self.dsems.append(d)
        return d

    def _deps(self, E, reads, writes, pw):
        for b in reads:
            for t in b.w.values():
                E.wait(t)
        for b in writes:
            for t in b.w.values():
                E.wait(t)
            for t in b.r.values():
                E.wait(t)
        for b in pw:
            for t in b.r.values():
                E.wait(t)

    def _upd(self, tok, reads, writes, pw):
        for b in reads:
            _add(b.r, tok)
        for b in writes:
            b.w = {tok.sid: tok}
            b.r = {}
        for b in pw:
            _add(b.w, tok)

    def op(self, E, fn, reads=(), writes=(), pw=(), inc=True):
        self._deps(E, reads, writes, pw)
        ins = fn(E.eng)
        if inc:
            ins.then_inc(E.sem, 1)
            E.count += 1
            tok = Tok(E.sid, E.sem, E.count, E)
        else:
            tok = Tok(E.sid, E.sem, E.count + 1, E)
        self._upd(tok, reads, writes, pw)
        return tok

    def dma(self, Q, out, in_, ds, reads=(), writes=(), pw=(), **kw):
        self._deps(Q, reads, writes, pw)
        Q.eng.dma_start(out=out, in_=in_, **kw).then_inc(ds.sem, 16)
        ds.count += 1
        tok = Tok(ds.sid, ds.sem, ds.count * 16, None)
        self._upd(tok, reads, writes, pw)
        return tok

    def barrier(self):
        toks = [Tok(e.sid, e.sem, e.count, e) for e in self.engs if e.count > 0]
        toks += [Tok(d.sid, d.sem, d.count * d.step, None) for d in self.dsems if d.count > 0]
        for e in self.engs:
            for t in toks:
                if t.eng is not e:
                    e.wait(t)

    def sb(self, st, name, shape, dtype, dma=False, nbs=0):
        self.nsid += 1
        name = f"{name}_u{self.nsid}"
        t = st.enter_context(self.nc.sbuf_tensor(name, shape, dtype))
        return TB(t, Buf(name), self.dsem(name) if dma else None, [Buf(f"{name}.{i}") for i in range(nbs)])

    def ps(self, st, name, shape=(128, 512), dtype=F32, nbs=0):
        self.nsid += 1
        name = f"{name}_u{self.nsid}"
        t = st.enter_context(self.nc.psum_tensor(name, list(shape), dtype))
        return TB(t, Buf(name), None, [Buf(f"{name}.{i}") for i in range(nbs)])


class Rot:
    def __init__(self, items):
        self.items, self.i = items, 0

    def next(self):
        x = self.items[self.i % len(self.items)]
        self.i += 1
        return x


class Prog:
    def __init__(self, steps, final):
        self.steps = list(steps)
        self.layers = sorted(set(i for (_, i) in self.steps))
        self.final = final
        nc = self.nc = bass.Bass("TRN2", target_bir_lowering=False)
        dt = nc.dram_tensor

        def inp(name, shape):
            return dt(name, list(shape), F32, kind="ExternalInput").ap()

        self.xin = inp("xin", [D, NTOK])
        self.cvec = inp("cvec", [128, NCH, 2])
        self.consts = inp("consts", [128, 128 + 128 + 512 + 512 + 512 + 64 + 2])
        self.ada_w = inp("ada_w", [4, D, 6 * D])
        self.adab = inp("adab", [128, 4, 48])
        self.nrm = inp("nrm", [128, 9, NCH])
        self.mlp_w1 = inp("mlp_w1", [4, D, DFF])
        self.mlp_w2 = inp("mlp_w2", [4, DFF, D])
        self.hgrn_w_in = inp("hgrn_w_in", [2, D, 5 * D])
        self.lbv = inp("lbv", [128, 2, 2, NCH])
        self.gnorm = inp("gnorm", [128, 2, NCH])
        self.hgrn_w_out = inp("hgrn_w_out", [2, D, D])
        self.conv_w_in = inp("conv_w_in", [2, D, 3 * D])
        self.convw = inp("convw", [128, 2, 3, NCH])
        self.convb = inp("convb", [128, 2, NCH])
        self.conv_w_out = inp("conv_w_out", [2, D, D])
        if final:
            self.yout = dt("yout", [D, NL], F32, kind="ExternalOutput").ap()
        else:
            self.yout = dt("yout", [D, NTOK], F32, kind="ExternalOutput").ap()
        self.xs = dt("xs", [D, NTOK], F32).ap()
        self.o1s = dt("o1s", [D, NTOK], F32).ap()
        self.zs = dt("zs", [D, NL + 64], F32).ap()
        self.gbs = dt("gbs", [D, NL], F32).ap()
        self.cc_in = dt("cc_in", [128, 1024], F32)
        self.cc_out = dt("cc_out", [256, 1024], F32)
        self.cch_in = dt("cch_in", [128, 512], F32)
        self.cch_out = dt("cch_out", [256, 512], F32)
        self.wsc = {}
        self.wsb = {}

        def wscratch(key, Kdim, N, NBc):
            self.wsc[key] = dt("ws_" + key, [N // NBc, 128, Kdim // 128, NBc], BF16).ap()
            self.wsb[key] = Buf("ws_" + key)

        for (kind, i) in self.steps:
            j = i // 2
            if kind == "mlp":
                wscratch(f"w1_{i}", D, DFF, 512)
                wscratch(f"w2_{i}", DFF, D, 128)
            elif i % 2 == 0:
                wscratch(f"hin_{j}", D, 5 * D, 512)
                wscratch(f"hout_{j}", D, D, 128)
            else:
                wscratch(f"cin_{j}", D, 3 * D, 512)
                wscratch(f"cout_{j}", D, D, 128)
        self.jobs_lat = [(q * T, T, 0) for q in range(NL // T)]
        self.job_ctx = (NL, CTX, 1)
        self.xsb = {c0: Buf(f"xs{c0}") for (c0, _, _) in self.jobs_lat + [self.job_ctx]}
        self.o1b = {c0: Buf(f"o1{c0}") for (c0, _, _) in self.jobs_lat + [self.job_ctx]}
        self.zsb = Buf("zs")
        self.gbb = Buf("gbs")
        self.ccb_in = Buf("ccin")
        self.ccb_out = Buf("ccout")
        with ExitStack() as st:
            self.k = K(nc, st)
            self.ccsem = self.k.dsem("cc")
            self.ccsem.step = 1
            self.build(st)

    def xs_ap(self, dram, c0, Tn):
        return dram[:, c0:c0 + Tn].rearrange("(c p) t -> p c t", p=128)

    def build(self, st):
        k = self.k
        nc = self.nc
        self.cst = k.sb(st, "cst", [128, 128 + 128 + 512 + 512 + 512 + 64 + 2], F32, dma=True)
        k.dma(k.sp, self.cst.t[:], self.consts[:, :], self.cst.ds, writes=[self.cst.b])
        c = self.cst.t
        self.ident_f = c[:, 0:128]
        self.ones_f = c[:, 128:256]
        self.smask = c[:, 256:768]
        self.sel = c[:, 1856:1858]
        self.identb = k.sb(st, "identb", [128, 128], BF16)
        self.tri = [k.sb(st, f"tri{d}", [128, 512], I32) for d in range(2)]
        k.op(k.dve, lambda e: e.tensor_copy(out=self.identb.t[:], in_=c[:, 0:128]), reads=[self.cst.b], writes=[self.identb.b])
        for d in range(2):
            k.op(k.dve, lambda e, d=d: e.tensor_copy(out=self.tri[d].t[:], in_=c[:, 768 + 512 * d:768 + 512 * (d + 1)]),
                 reads=[self.cst.b], writes=[self.tri[d].b])
        self.epsb = k.sb(st, "epsb", [128, 1], F32)
        k.op(k.dve, lambda e: e.memset(self.epsb.t[:], EPS), writes=[self.epsb.b])
        self.par = k.sb(st, "par", [128, 4 * 48 + 9 * 8 + 32 + 16 + 48 + 16 + 16], F32, dma=True)
        p = self.par.t
        o = 0
        self.adab_s = p[:, o:o + 192].rearrange("p (i c) -> p i c", i=4); o += 192
        self.nrm_s = p[:, o:o + 72].rearrange("p (i c) -> p i c", i=9); o += 72
        self.lbv_s = p[:, o:o + 32].rearrange("p (d j c) -> p d j c", d=2, j=2); o += 32
        self.gn_s = p[:, o:o + 16].rearrange("p (j c) -> p j c", j=2); o += 16
        self.cw_s = p[:, o:o + 48].rearrange("p (j t c) -> p j t c", j=2, t=3); o += 48
        self.cb_s = p[:, o:o + 16].rearrange("p (j c) -> p j c", j=2); o += 16
        self.cv_s = p[:, o:o + 16].rearrange("p (c j) -> p c j", j=2); o += 16
        for dst, src in ((self.adab_s, self.adab), (self.nrm_s, self.nrm), (self.lbv_s, self.lbv), (self.gn_s, self.gnorm),
                         (self.cw_s, self.convw), (self.cb_s, self.convb), (self.cv_s, self.cvec)):
            k.dma(k.sp, dst, src, self.par.ds, pw=[self.par.b])
        self.setup_ada(st)
        self.convert_weights()
        cp = k.dsem("cpin")
        k.dma(k.sp, self.xs[:, :], self.xin[:, :], cp, pw=list(self.xsb.values()))
        k.barrier()
        for (kind, i) in self.steps:
            j = i // 2
            ctx_live = i < 2
            if kind == "mlp":
                jobs = ([self.job_ctx] if ctx_live else []) + self.jobs_lat
                self.mlp_layer(i, jobs)
            elif i % 2 == 0:
                self.hgrn_layer(i, j, ctx_live)
            else:
                self.conv_layer(i, j, ctx_live)
            k.barrier()
        if self.final:
            self.final_norm()
        else:
            cpo = k.dsem("cpout")
            k.dma(k.sp, self.yout[:, :], self.xs[:, :], cpo, reads=list(self.xsb.values()))
        k.barrier()

    def setup_ada(self, st):
        k = self.k
        self.adaT = k.sb(st, "adaT", [128, 4, 48, 2], F32)
        self.A1 = k.sb(st, "A1", [128, 4, NCH, 2], F32)
        self.A2 = k.sb(st, "A2", [128, 4, NCH, 2], F32)
        self.scv = k.sb(st, "scv", [128, NCH, 2], F32)
        k.op(k.act, lambda e: e.activation(out=self.scv.t[:], in_=self.cv_s, func=AF.Silu), reads=[self.par.b], writes=[self.scv.b])
        with ExitStack() as s2:
            wsl = [k.sb(s2, f"adaw{q}", [128, NCH, 512], F32, dma=True) for q in range(2)]
            pa = k.ps(s2, "ps_ada", (128, 512), F32)
            rot = Rot(wsl)
            for i in self.layers:
                for nb in range(12):
                    w = rot.next()
                    k.dma(k.sp, w.t[:], self.ada_w[i, :, nb * 512:(nb + 1) * 512].rearrange("(c p) n -> p c n", p=128), w.ds, writes=[w.b])
                    for cc in range(4):
                        ch = nb * 4 + cc
                        for kc in range(NCH):
                            k.op(k.pe, lambda e, w=w, cc=cc, kc=kc, ch=ch: e.matmul(
                                pa.t[:, ch * 2:ch * 2 + 2], lhsT=w.t[:, kc, cc * 128:(cc + 1) * 128], rhs=self.scv.t[:, kc, :],
                                start=(kc == 0), stop=(kc == NCH - 1)),
                                reads=[w.b, self.scv.b], pw=[pa.b], inc=(kc == NCH - 1))
                k.op(k.dve, lambda e, i=i: e.tensor_tensor(
                    out=self.adaT.t[:, i], in0=pa.t[:, 0:96].rearrange("p (c j) -> p c j", j=2),
                    in1=self.adab_s[:, i].unsqueeze(2).to_broadcast([128, 48, 2]), op=ALU.add),
                    reads=[pa.b, self.par.b], pw=[self.adaT.b])
                for (A, nidx, split) in ((self.A1, i, 1), (self.A2, 4 + i, 4)):
                    k.op(k.dve, lambda e, A=A, split=split, i=i: e.tensor_scalar(
                        out=A.t[:, i], in0=self.adaT.t[:, i, split * 8:(split + 1) * 8, :], scalar1=1.0, scalar2=None, op0=ALU.add),
                        reads=[self.adaT.b], pw=[A.b])
                    k.op(k.dve, lambda e, A=A, nidx=nidx, i=i: e.tensor_tensor(
                        out=A.t[:, i], in0=A.t[:, i], in1=self.nrm_s[:, nidx].unsqueeze(2).to_broadcast([128, NCH, 2]), op=ALU.mult),
                        reads=[self.par.b, A.b], pw=[A.b])
            k.barrier()

    def ada(self, i, split, c, s):
        return self.adaT.t[:, i, split * 8 + c, s:s + 1]

    def convert_weights(self):
        k = self.k
        with ExitStack() as s2:
            stg = [k.sb(s2, f"stg{q}", [128, 5 * D], BF16, dma=True) for q in range(2)]
            for q_ in stg:
                q_.ds2 = k.dsem("stgst")
            rot = Rot(stg)

            def conv(key, src, Kdim, N):
                dst = self.wsc[key]
                nbc = dst.shape[3]
                for kc in range(Kdim // 128):
                    s = rot.next()
                    k.dma(k.pool, s.t[:, 0:N], src[kc * 128:(kc + 1) * 128, :], s.ds, writes=[s.b], max_dma_last_dim=4096)
                    k.dma(k.sp, dst[:, :, kc, :].rearrange("nb p n -> p nb n"), s.t[:, 0:N].rearrange("p (nb n) -> p nb n", n=nbc),
                          s.ds2, reads=[s.b], pw=[self.wsb[key]])

            for (kind, i) in self.steps:
                j = i // 2
                if kind == "mlp":
                    conv(f"w1_{i}", self.mlp_w1[i], D, DFF)
                    conv(f"w2_{i}", self.mlp_w2[i], DFF, D)
                elif i % 2 == 0:
                    conv(f"hin_{j}", self.hgrn_w_in[j], D, 5 * D)
                    conv(f"hout_{j}", self.hgrn_w_out[j], D, D)
                else:
                    conv(f"cin_{j}", self.conv_w_in[j], D, 3 * D)
                    conv(f"cout_{j}", self.conv_w_out[j], D, D)
            k.barrier()

    def norm_mod(self, x, Tn, hT, big, rt, rstd, psb, A, i, shsplit, s, nidx_unused=None):
        k = self.k
        k.op(k.act, lambda e: e.activation(out=big.t[:, :, :Tn], in_=x.t[:, :, :Tn], func=AF.Square), reads=[x.b], writes=[big.b])
        for c in range(NCH):
            k.op(k.pe, lambda e, c=c: e.matmul(psb.t[:, :Tn], lhsT=self.ones_f, rhs=big.t[:, c, :Tn], start=(c == 0), stop=(c == NCH - 1)),
                 reads=[big.b, self.cst.b], writes=[psb.b] if c == 0 else [], pw=[psb.b] if c else [], inc=(c == NCH - 1))
        k.op(k.act, lambda e: e.activation(out=rt.t[:, :Tn], in_=psb.t[:, :Tn], func=AF.Sqrt, bias=self.epsb.t[:, 0:1], scale=1.0 / D),
             reads=[psb.b, self.epsb.b], writes=[rt.b])
        k.op(k.dve, lambda e: e.reciprocal(out=rstd.t[:, :Tn], in_=rt.t[:, :Tn]), reads=[rt.b], writes=[rstd.b])
        k.op(k.dve, lambda e: e.tensor_tensor(out=big.t[:, :, :Tn], in0=x.t[:, :, :Tn],
                                              in1=rstd.t[:, :Tn].unsqueeze(1).to_broadcast([128, NCH, Tn]), op=ALU.mult),
             reads=[x.b, rstd.b], writes=[big.b])
        for c in range(NCH):
            k.op(k.pool, lambda e, c=c: e.tensor_scalar(out=hT.t[:, c, :Tn], in0=big.t[:, c, :Tn], scalar1=A.t[:, i, c, s:s + 1],
                                                         scalar2=self.ada(i, shsplit, c, s), op0=ALU.mult, op1=ALU.add),
                 reads=[big.b, A.b, self.adaT.b], writes=[hT.b] if c == 0 else [], pw=[hT.b] if c else [])

    def mlp_layer(self, i, jobs):
        k = self.k
        w1s, w2s = self.wsc[f"w1_{i}"], self.wsc[f"w2_{i}"]
        w1b, w2b = self.wsb[f"w1_{i}"], self.wsb[f"w2_{i}"]
        with ExitStack() as st:
            xt = Rot([k.sb(st, f"m_x{q}", [128, NCH, T], F32, dma=True) for q in range(2)])
            hT = k.sb(st, "m_hT", [128, NCH, T], BF16)
            big = k.sb(st, "m_big", [128, NCH, T], F32)
            rt = k.sb(st, "m_rt", [128, T], F32)
            rstd = k.sb(st, "m_rstd", [128, T], F32)
            hid = k.sb(st, "m_hid", [128, 32, T], BF16, nbs=32)
            rb = Rot([k.sb(st, f"m_r{q}", [128, T], F32) for q in range(3)])
            ws = Rot([k.sb(st, f"m_w{q}", [128, 4096], BF16, dma=True) for q in range(3)])
            ps = Rot([k.ps(st, f"m_ps{q}") for q in range(6)])
            psn = k.ps(st, "m_psn")
            for (c0, Tn, s) in jobs:
                x = xt.next()
                k.dma(k.sp, x.t[:, :, :Tn], self.xs_ap(self.xs, c0, Tn), x.ds, reads=[self.xsb[c0]], writes=[x.b])
                self.norm_mod(x, Tn, hT, big, rt, rstd, psn, self.A2, i, 3, s)
                for nb in range(8):
                    w = ws.next()
                    k.dma(k.sp, w.t[:, :], w1s[nb].rearrange("p k n -> p (k n)"), w.ds, reads=[w1b], writes=[w.b])
                    for nn in range(4):
                        n = nb * 4 + nn
                        p = ps.next()
                        for kc in range(NCH):
                            k.op(k.pe, lambda e, w=w, p=p, kc=kc, nn=nn: e.matmul(
                                p.t[:, :Tn], lhsT=w.t[:, kc * 512 + nn * 128:kc * 512 + (nn + 1) * 128], rhs=hT.t[:, kc, :Tn],
                                start=(kc == 0), stop=(kc == NCH - 1)),
                                reads=[w.b, hT.b], writes=[p.b] if kc == 0 else [], pw=[p.b] if kc else [], inc=(kc == NCH - 1))
                        r = rb.next()
                        k.op(k.act, lambda e, r=r, p=p: e.activation(out=r.t[:, :Tn], in_=p.t[:, :Tn], func=AF.Relu), reads=[p.b], writes=[r.b])
                        k.op(k.pool, lambda e, r=r, n=n: e.tensor_tensor(out=hid.t[:, n, :Tn], in0=r.t[:, :Tn], in1=r.t[:, :Tn], op=ALU.mult),
                             reads=[r.b], writes=[hid.bs[n]])
                for c in range(NCH):
                    w = ws.next()
                    k.dma(k.sp, w.t[:, :], w2s[c].rearrange("p k n -> p (k n)"), w.ds, reads=[w2b], writes=[w.b])
                    p = ps.next()
                    for kc in range(32):
                        k.op(k.pe, lambda e, w=w, p=p, kc=kc: e.matmul(
                            p.t[:, :Tn], lhsT=w.t[:, kc * 128:(kc + 1) * 128], rhs=hid.t[:, kc, :Tn], start=(kc == 0), stop=(kc == 31)),
                            reads=[w.b, hid.bs[kc]], writes=[p.b] if kc == 0 else [], pw=[p.b] if kc else [], inc=(kc == 31))
                    k.op(k.dve, lambda e, p=p, c=c, x=x: e.scalar_tensor_tensor(
                        out=x.t[:, c, :Tn], in0=p.t[:, :Tn], scalar=self.ada(i, 5, c, s), in1=x.t[:, c, :Tn], op0=ALU.mult, op1=ALU.add),
                        reads=[p.b, self.adaT.b, x.b], pw=[x.b])
                k.dma(k.sp, self.xs_ap(self.xs, c0, Tn), x.t[:, :, :Tn], x.ds, reads=[x.b], writes=[self.xsb[c0]])

    def final_norm(self):
        k = self.k
        with ExitStack() as st:
            xt = Rot([k.sb(st, f"f_x{q}", [128, NCH, T], F32, dma=True) for q in range(2)])
            big = Rot([k.sb(st, f"f_big{q}", [128, NCH, T], F32, dma=True) for q in range(2)])
            rt = k.sb(st, "f_rt", [128, T], F32)
            rstd = k.sb(st, "f_rstd", [128, T], F32)
            psn = k.ps(st, "f_psn")
            yb = Buf("yout")
            for (c0, Tn, s) in self.jobs_lat:
                x = xt.next()
                bg = big.next()
                k.dma(k.sp, x.t[:, :, :Tn], self.xs_ap(self.xs, c0, Tn), x.ds, reads=[self.xsb[c0]], writes=[x.b])
                k.op(k.act, lambda e: e.activation(out=bg.t[:, :, :Tn], in_=x.t[:, :, :Tn], func=AF.Square), reads=[x.b], writes=[bg.b])
                for c in range(NCH):
                    k.op(k.pe, lambda e, c=c: e.matmul(psn.t[:, :Tn], lhsT=self.ones_f, rhs=bg.t[:, c, :Tn], start=(c == 0), stop=(c == NCH - 1)),
                         reads=[bg.b, self.cst.b], writes=[psn.b] if c == 0 else [], pw=[psn.b] if c else [], inc=(c == NCH - 1))
                k.op(k.act, lambda e: e.activation(out=rt.t[:, :Tn], in_=psn.t[:, :Tn], func=AF.Sqrt, bias=self.epsb.t[:, 0:1], scale=1.0 / D),
                     reads=[psn.b, self.epsb.b], writes=[rt.b])
                k.op(k.dve, lambda e: e.reciprocal(out=rstd.t[:, :Tn], in_=rt.t[:, :Tn]), reads=[rt.b], writes=[rstd.b])
                k.op(k.dve, lambda e: e.tensor_tensor(out=bg.t[:, :, :Tn], in0=x.t[:, :, :Tn],
                                                      in1=rstd.t[:, :Tn].unsqueeze(1).to_broadcast([128, NCH, Tn]), op=ALU.mult),
                     reads=[x.b, rstd.b], writes=[bg.b])
                k.op(k.pool, lambda e: e.tensor_tensor(out=bg.t[:, :, :Tn], in0=bg.t[:, :, :Tn],
                                                       in1=self.nrm_s[:, 8].unsqueeze(2).to_broadcast([128, NCH, Tn]), op=ALU.mult),
                     reads=[bg.b, self.par.b], writes=[bg.b])
                k.dma(k.sp, self.xs_ap(self.yout, c0, Tn), bg.t[:, :, :Tn], bg.ds, reads=[bg.b], pw=[yb])


    def hgrn_layer(self, i, j, ctx_live):
        k = self.k
        wins, winb = self.wsc[f"hin_{j}"], self.wsb[f"hin_{j}"]
        wos, wob = self.wsc[f"hout_{j}"], self.wsb[f"hout_{j}"]
        with ExitStack() as st:
            x = k.sb(st, "h_x", [128, NCH, T], F32, dma=True)
            hT = k.sb(st, "h_hT", [128, NCH, T], BF16)
            big = k.sb(st, "h_big", [128, NCH, T], F32, dma=True)
            ot = k.sb(st, "h_ot", [128, NCH, T], F32, dma=True)
            rt = k.sb(st, "h_rt", [128, T], F32)
            rstd = k.sb(st, "h_rstd", [128, T], F32)
            tnames = ["sg", "g", "kk", "cum", "c2", "dd", "ll", "e1", "sq"]
            tsets = Rot([{n: k.sb(st, f"h_{n}{q}", [128, T], F32) for n in tnames} for q in range(2)])
            bsets = Rot([{n: k.sb(st, f"h_{n}{q}", [128, T], BF16) for n in ("qd", "kd", "kl")} for q in range(2)])
            qeT = k.sb(st, "h_qeT", [128, NCH, T], BF16, nbs=NCH)
            scT = k.sb(st, "h_scT", [128, NCH, T], BF16, nbs=NCH)
            kltm = k.sb(st, "h_kltm", [128, 4, NCH, 128], BF16, nbs=NCH)
            vtm = k.sb(st, "h_vtm", [128, 4, D], BF16)
            elast = k.sb(st, "h_elast", [128, NCH, 8], F32, nbs=NCH)
            S = k.sb(st, "h_S", [128, NCH, 128], F32, dma=True, nbs=NCH)
            Sb = k.sb(st, "h_Sb", [128, NCH, 128], BF16, nbs=NCH)
            on = k.sb(st, "h_on", [128, NCH, T], BF16, nbs=NCH)
            LB = k.sb(st, "h_LB", [128, 2, NCH], F32)
            OML = k.sb(st, "h_OML", [128, 2, NCH], F32)
            ws = Rot([k.sb(st, f"h_w{q}", [128, 4096], BF16, dma=True) for q in range(3)])
            wo = k.sb(st, "h_wo", [128, NCH, NCH, 128], BF16, dma=True)
            pp = Rot([k.ps(st, f"h_pp{q}") for q in range(2)])
            pT = k.ps(st, "h_pT", (128, 1024), BF16)
            pS = k.ps(st, "h_pS")
            pO = Rot([k.ps(st, f"h_pO{q}") for q in range(2)])
            pP = Rot([k.ps(st, f"h_pP{q}") for q in range(2)])
            k.dma(k.sp, wo.t[:], wos.rearrange("nb p k n -> p nb k n"), wo.ds, reads=[wob], writes=[wo.b])
            if j == 0:
                k.op(k.dve, lambda e: e.memset(LB.t[:], 0.0), writes=[LB.b])
                k.op(k.dve, lambda e: e.memset(OML.t[:], 1.0), writes=[OML.b])
            else:
                k.op(k.dve, lambda e: e.tensor_tensor(out=LB.t[:], in0=self.lbv_s[:, :, 1, :], in1=self.lbv_s[:, :, 0, :], op=ALU.subtract),
                     reads=[self.par.b], writes=[LB.b])
                k.op(k.act, lambda e: e.activation(out=LB.t[:], in_=LB.t[:], func=AF.Sigmoid), reads=[LB.b], writes=[LB.b])
                k.op(k.dve, lambda e: e.tensor_scalar(out=OML.t[:], in0=LB.t[:], scalar1=-1.0, scalar2=1.0, op0=ALU.mult, op1=ALU.add),
                     reads=[LB.b], writes=[OML.b])

            def loadw(nb):
                w = ws.next()
                k.dma(k.sp, w.t[:, :], wins[nb].rearrange("p k n -> p (k n)"), w.ds, reads=[winb], writes=[w.b])
                return w

            def proj_chunk(w, nn, Tn):
                p = pp.next()
                for kc in range(NCH):
                    k.op(k.pe, lambda e: e.matmul(p.t[:, :Tn], lhsT=w.t[:, kc * 512 + nn * 128:kc * 512 + (nn + 1) * 128], rhs=hT.t[:, kc, :Tn],
                                                  start=(kc == 0), stop=(kc == NCH - 1)),
                         reads=[w.b, hT.b], writes=[p.b] if kc == 0 else [], pw=[p.b] if kc else [], inc=(kc == NCH - 1))
                return p

            def bc(ap3, nch):
                return ap3.to_broadcast([128, nch, 64])

            def zero_state():
                k.op(k.dve, lambda e: e.memset(S.t[:], 0.0), writes=S.bs)
                k.op(k.dve, lambda e: e.memset(Sb.t[:], 0.0), writes=Sb.bs)

            for d in range(2):
                Ridx, Lidx = (31, 63) if d == 0 else (32, 0)
                k.op(k.pool, lambda e: e.memset(scT.t[:], 0.0), writes=scT.bs)
                if d == 0:
                    zero_state()
                    jobs = [self.job_ctx] + self.jobs_lat
                else:
                    jobs = self.jobs_lat[::-1] + ([self.job_ctx] if ctx_live else [])
                for (c0, Tn, s) in jobs:
                    nch, nsub = Tn // 64, Tn // 128
                    if d == 1 and s == 1:
                        zero_state()
                    need_o = not (s == 1 and not ctx_live)
                    k.dma(k.sp, x.t[:, :, :Tn], self.xs_ap(self.xs, c0, Tn), x.ds, reads=[self.xsb[c0]], writes=[x.b])
                    self.norm_mod(x, Tn, hT, big, rt, rstd, pp.next(), self.A1, i, 0, s)
                    for hg in range(2):
                        wf = loadw(2 * d + hg)
                        wq = loadw(6 + hg)
                        for hh in range(4):
                            hd = hg * 4 + hh
                            t = tsets.next()
                            b = bsets.next()
                            sg, g, kk, cum, c2, dd, ll, e1, sq = (t[n] for n in tnames)
                            pF = proj_chunk(wf, hh, Tn)
                            k.op(k.act, lambda e: e.activation(out=sg.t[:, :Tn], in_=pF.t[:, :Tn], func=AF.Sigmoid), reads=[pF.b], writes=[sg.b])
                            pQ = proj_chunk(wq, hh, Tn)
                            k.op(k.act, lambda e: e.activation(out=sq.t[:, :Tn], in_=pQ.t[:, :Tn], func=AF.Silu), reads=[pQ.b], writes=[sq.b])
                            k.op(k.dve, lambda e: e.tensor_scalar(out=sg.t[:, :Tn], in0=sg.t[:, :Tn], scalar1=OML.t[:, d, hd:hd + 1], scalar2=LB.t[:, d, hd:hd + 1],
                                                                  op0=ALU.mult, op1=ALU.add), reads=[sg.b, OML.b, LB.b], writes=[sg.b])
                            k.op(k.act, lambda e: e.activation(out=g.t[:, :Tn], in_=sg.t[:, :Tn], func=AF.Ln), reads=[sg.b], writes=[g.b])
                            k.op(k.pool, lambda e: e.tensor_scalar(out=kk.t[:, :Tn], in0=sg.t[:, :Tn], scalar1=-1.0, scalar2=1.0, op0=ALU.mult, op1=ALU.add),
                                 reads=[sg.b], writes=[kk.b])
                            k.op(k.dve, lambda e: e.tensor_tensor_scan(out=cum.t[:, :Tn], data0=self.smask[:, :Tn], data1=g.t[:, :Tn], initial=0.0,
                                                                       op0=ALU.mult, op1=ALU.add), reads=[g.b, self.cst.b], writes=[cum.b])
                            cm = cum
                            if d == 1:
                                cum3 = cum.t[:, :Tn].rearrange("p (n m) -> p n m", m=64)
                                k.op(k.pool, lambda e: e.tensor_tensor(out=sg.t[:, :Tn], in0=g.t[:, :Tn], in1=cum.t[:, :Tn], op=ALU.subtract),
                                     reads=[g.b, cum.b], writes=[sg.b])
                                k.op(k.pool, lambda e: e.tensor_tensor(out=c2.t[:, :Tn].rearrange("p (n m) -> p n m", m=64),
                                                                       in0=sg.t[:, :Tn].rearrange("p (n m) -> p n m", m=64),
                                                                       in1=bc(cum3[:, :, 63:64], nch), op=ALU.add),
                                     reads=[sg.b, cum.b], writes=[c2.b])
                                cm = c2
                            cm3 = cm.t[:, :Tn].rearrange("p (n m) -> p n m", m=64)
                            k.op(k.dve, lambda e: e.tensor_tensor(out=dd.t[:, :Tn].rearrange("p (n m) -> p n m", m=64), in0=cm3,
                                                                  in1=bc(cm3[:, :, Ridx:Ridx + 1], nch), op=ALU.subtract), reads=[cm.b], writes=[dd.b])
                            k.op(k.pool, lambda e: e.tensor_tensor(out=ll.t[:, :Tn].rearrange("p (n m) -> p n m", m=64), in0=cm3,
                                                                   in1=bc(cm3[:, :, Lidx:Lidx + 1], nch), op=ALU.subtract), reads=[cm.b], writes=[ll.b])
                            k.op(k.act, lambda e: e.activation(out=e1.t[:, :Tn], in_=dd.t[:, :Tn], func=AF.Exp), reads=[dd.b], writes=[e1.b])
                            k.op(k.act, lambda e: e.activation(out=dd.t[:, :Tn], in_=dd.t[:, :Tn], func=AF.Exp, scale=-1.0), reads=[dd.b], writes=[dd.b])
                            k.op(k.act, lambda e: e.activation(out=g.t[:, :Tn], in_=cm.t[:, :Tn], func=AF.Exp), reads=[cm.b], writes=[g.b])
                            k.op(k.act, lambda e: e.activation(out=ll.t[:, :Tn], in_=ll.t[:, :Tn], func=AF.Exp, scale=-1.0), reads=[ll.b], writes=[ll.b])
                            k.op(k.act, lambda e: e.activation(out=elast.t[:, hd, :nch].unsqueeze(2), in_=cm3[:, :, Lidx:Lidx + 1], func=AF.Exp),
                                 reads=[cm.b], writes=[elast.bs[hd]])
                            k.op(k.dve, lambda e: e.tensor_tensor(out=b["qd"].t[:, :Tn], in0=sq.t[:, :Tn], in1=e1.t[:, :Tn], op=ALU.mult),
                                 reads=[sq.b, e1.b], writes=[b["qd"].b])
                            k.op(k.pool, lambda e: e.tensor_tensor(out=b["kd"].t[:, :Tn], in0=kk.t[:, :Tn], in1=dd.t[:, :Tn], op=ALU.mult),
                                 reads=[kk.b, dd.b], writes=[b["kd"].b])
                            k.op(k.dve, lambda e: e.tensor_tensor(out=qeT.t[:, hd, :Tn], in0=sq.t[:, :Tn], in1=g.t[:, :Tn], op=ALU.mult),
                                 reads=[sq.b, g.b], writes=[qeT.bs[hd]])
                            k.op(k.pool, lambda e: e.tensor_tensor(out=b["kl"].t[:, :Tn], in0=kk.t[:, :Tn], in1=ll.t[:, :Tn], op=ALU.mult),
                                 reads=[kk.b, ll.b], writes=[b["kl"].b])
                            for sub in range(nsub):
                                k.op(k.pe, lambda e: e.transpose(pT.t[:, sub * 128:(sub + 1) * 128], b["kl"].t[:, sub * 128:(sub + 1) * 128], self.identb.t[:]),
                                     reads=[b["kl"].b, self.identb.b], writes=[pT.b] if sub == 0 else [], pw=[pT.b] if sub else [], inc=(sub == nsub - 1))
                            k.op(k.act, lambda e: e.activation(out=kltm.t[:, :nsub, hd, :], in_=pT.t[:, :nsub * 128].rearrange("p (s n) -> p s n", n=128), func=AF.Copy),
                                 reads=[pT.b], writes=[kltm.bs[hd]])
                            for sub in range(nsub):
                                sl = slice(sub * 128, (sub + 1) * 128)
                                k.op(k.pe, lambda e: e.matmul(pS.t[:, sl], lhsT=b["kd"].t[:, sl], rhs=b["qd"].t[:, sl], start=True, stop=True),
                                     reads=[b["kd"].b, b["qd"].b], writes=[pS.b] if sub == 0 else [], pw=[pS.b] if sub else [], inc=(sub == nsub - 1))
                            k.op(k.dve, lambda e: e.copy_predicated(out=scT.t[:, hd, :Tn], mask=self.tri[d].t[:, :Tn], data=pS.t[:, :Tn]),
                                 reads=[pS.b, self.tri[d].b], pw=[scT.bs[hd]])
                    wv = [loadw(4), loadw(5)]
                    for sub in range(nsub):
                        for half in range(2):
                            p = pp.next()
                            for kc in range(NCH):
                                k.op(k.pe, lambda e: e.matmul(p.t[:, :], lhsT=hT.t[:, kc, sub * 128:(sub + 1) * 128], rhs=wv[half].t[:, kc * 512:(kc + 1) * 512],
                                                              start=(kc == 0), stop=(kc == NCH - 1)),
                                     reads=[wv[half].b, hT.b], writes=[p.b] if kc == 0 else [], pw=[p.b] if kc else [], inc=(kc == NCH - 1))
                            k.op(k.act, lambda e: e.activation(out=vtm.t[:, sub, half * 512:(half + 1) * 512], in_=p.t[:, :], func=AF.Copy),
                                 reads=[p.b], writes=[vtm.b] if (sub == 0 and half == 0) else [], pw=[] if (sub == 0 and half == 0) else [vtm.b])
                    order = list(range(nch)) if d == 0 else list(range(nch))[::-1]
                    for ci in order:
                        sub, hf = divmod(ci, 2)
                        pr = slice(hf * 64, hf * 64 + 64)
                        po = pO.next()
                        for hd in range(NCH):
                            if need_o:
                                k.op(k.pe, lambda e: e.matmul(po.t[:, hd * 64:(hd + 1) * 64], lhsT=Sb.t[:, hd, :], rhs=qeT.t[:, hd, ci * 64:(ci + 1) * 64],
                                                              start=True, stop=False), reads=[Sb.bs[hd], qeT.bs[hd]], pw=[po.b], inc=False)
                                k.op(k.pe, lambda e: e.matmul(po.t[:, hd * 64:(hd + 1) * 64], lhsT=vtm.t[pr, sub, hd * 128:(hd + 1) * 128],
                                                              rhs=scT.t[pr, hd, sub * 128 + hf * 64:sub * 128 + hf * 64 + 64], start=False, stop=True),
                                     reads=[vtm.b, scT.bs[hd]], pw=[po.b], inc=True)
                            if hd % 4 == 0:
                                p4 = pP.next()
                            k.op(k.pe, lambda e: e.matmul(p4.t[:, (hd % 4) * 128:(hd % 4 + 1) * 128], lhsT=kltm.t[pr, sub, hd, :],
                                                          rhs=vtm.t[pr, sub, hd * 128:(hd + 1) * 128], start=True, stop=True),
                                 reads=[kltm.bs[hd], vtm.b], pw=[p4.b], inc=True)
                            k.op(k.dve, lambda e: e.scalar_tensor_tensor(out=S.t[:, hd, :], in0=S.t[:, hd, :], scalar=elast.t[:, hd, ci:ci + 1],
                                                                         in1=p4.t[:, (hd % 4) * 128:(hd % 4 + 1) * 128], op0=ALU.mult, op1=ALU.add),
                                 reads=[S.bs[hd], elast.bs[hd], p4.b], writes=[S.bs[hd]])
                            k.op(k.act, lambda e: e.activation(out=Sb.t[:, hd, :], in_=S.t[:, hd, :], func=AF.Copy), reads=[S.bs[hd]], writes=[Sb.bs[hd]])
                        if need_o:
                            k.op(k.act, lambda e: e.activation(out=ot.t[:, :, ci * 64:(ci + 1) * 64], in_=po.t[:, :].rearrange("p (h m) -> p h m", m=64), func=AF.Copy),
                                 reads=[po.b], pw=[ot.b])
                    if d == 0:
                        if need_o:
                            k.dma(k.sp, self.xs_ap(self.o1s, c0, Tn), ot.t[:, :, :Tn], ot.ds, reads=[ot.b], writes=[self.o1b[c0]])
                    else:
                        k.dma(k.sp, big.t[:, :, :Tn], self.xs_ap(self.o1s, c0, Tn), big.ds, reads=[self.o1b[c0]], writes=[big.b])
                        k.op(k.dve, lambda e: e.tensor_tensor(out=ot.t[:, :, :Tn], in0=ot.t[:, :, :Tn], in1=big.t[:, :, :Tn], op=ALU.add),
                             reads=[ot.b, big.b], writes=[ot.b])
                        k.op(k.act, lambda e: e.activation(out=big.t[:, :, :Tn], in_=ot.t[:, :, :Tn], func=AF.Square), reads=[ot.b], writes=[big.b])
                        for hg in range(2):
                            wg = loadw(8 + hg)
                            for hh in range(4):
                                hd = hg * 4 + hh
                                t = tsets.next()
                                pn = pp.next()
                                k.op(k.pe, lambda e: e.matmul(pn.t[:, :Tn], lhsT=self.ones_f, rhs=big.t[:, hd, :Tn], start=True, stop=True),
                                     reads=[big.b, self.cst.b], writes=[pn.b])
                                k.op(k.act, lambda e: e.activation(out=t["sg"].t[:, :Tn], in_=pn.t[:, :Tn], func=AF.Sqrt, bias=self.epsb.t[:, 0:1], scale=1.0 / 128),
                                     reads=[pn.b, self.epsb.b], writes=[t["sg"].b])
                                k.op(k.dve, lambda e: e.reciprocal(out=t["g"].t[:, :Tn], in_=t["sg"].t[:, :Tn]), reads=[t["sg"].b], writes=[t["g"].b])
                                pg = proj_chunk(wg, hh, Tn)
                                k.op(k.act, lambda e: e.activation(out=t["sq"].t[:, :Tn], in_=pg.t[:, :Tn], func=AF.Silu), reads=[pg.b], writes=[t["sq"].b])
                                k.op(k.dve, lambda e: e.scalar_tensor_tensor(out=t["kk"].t[:, :Tn], in0=ot.t[:, hd, :Tn], scalar=self.gn_s[:, j, hd:hd + 1],
                                                                             in1=t["g"].t[:, :Tn], op0=ALU.mult, op1=ALU.mult),
                                     reads=[ot.b, self.par.b, t["g"].b], writes=[t["kk"].b])
                                k.op(k.pool, lambda e: e.tensor_tensor(out=on.t[:, hd, :Tn], in0=t["kk"].t[:, :Tn], in1=t["sq"].t[:, :Tn], op=ALU.mult),
                                     reads=[t["kk"].b, t["sq"].b], writes=[on.bs[hd]])
                        for c in range(NCH):
                            p = pp.next()
                            for kc in range(NCH):
                                k.op(k.pe, lambda e: e.matmul(p.t[:, :Tn], lhsT=wo.t[:, c, kc, :], rhs=on.t[:, kc, :Tn], start=(kc == 0), stop=(kc == NCH - 1)),
                                     reads=[wo.b, on.bs[kc]], writes=[p.b] if kc == 0 else [], pw=[p.b] if kc else [], inc=(kc == NCH - 1))
                            k.op(k.dve, lambda e: e.scalar_tensor_tensor(out=x.t[:, c, :Tn], in0=p.t[:, :Tn], scalar=self.ada(i, 2, c, s), in1=x.t[:, c, :Tn],
                                                                         op0=ALU.mult, op1=ALU.add),
                                 reads=[p.b, self.adaT.b, x.b], pw=[x.b])
                        k.dma(k.sp, self.xs_ap(self.xs, c0, Tn), x.t[:, :, :Tn], x.ds, reads=[x.b], writes=[self.xsb[c0]])
                if d == 0:
                    k.dma(k.pool, self.cc_in.ap(), S.t[:].rearrange("p h v -> p (h v)"), S.ds, reads=S.bs, writes=[self.ccb_in])
                    self.collective(self.cc_in, self.cc_out)
                    hbv = big.t[:, 0:4, :].rearrange("p (r a) t -> p r (a t)", r=2)
                    k.dma(k.pool, hbv, self.cc_out.ap().rearrange("(r p) n -> p r n", p=128), S.ds, reads=[self.ccb_out], writes=[big.b])
                    k.op(k.dve, lambda e: e.tensor_scalar(out=S.t[:].rearrange("p h v -> p (h v)"), in0=hbv[:, 0, :], scalar1=self.sel[:, 0:1], scalar2=None, op0=ALU.mult),
                         reads=[big.b, self.cst.b], writes=S.bs)
                    k.op(k.dve, lambda e: e.scalar_tensor_tensor(out=S.t[:].rearrange("p h v -> p (h v)"), in0=hbv[:, 1, :], scalar=self.sel[:, 1:2],
                                                                 in1=S.t[:].rearrange("p h v -> p (h v)"), op0=ALU.mult, op1=ALU.add),
                         reads=[big.b, self.cst.b] + S.bs, writes=S.bs)
                    k.op(k.act, lambda e: e.activation(out=Sb.t[:], in_=S.t[:], func=AF.Copy), reads=S.bs, writes=Sb.bs)


    def conv_layer(self, i, j, ctx_live):
        k = self.k
        axis2 = (j % 2 == 0)
        wins, winb = self.wsc[f"cin_{j}"], self.wsb[f"cin_{j}"]
        wos, wob = self.wsc[f"cout_{j}"], self.wsb[f"cout_{j}"]
        cw = lambda tap, c: self.cw_s[:, j, tap, c:c + 1]
        cb = lambda c: self.cb_s[:, j, c:c + 1]
        with ExitStack() as st:
            xt = Rot([k.sb(st, f"c_x{q}", [128, NCH, T], F32, dma=True) for q in range(1)])
            hT = k.sb(st, "c_hT", [128, NCH, T], BF16)
            big = k.sb(st, "c_big", [128, NCH, T], F32)
            rt = k.sb(st, "c_rt", [128, T], F32)
            rstd = k.sb(st, "c_rstd", [128, T], F32)
            zh = Rot([k.sb(st, f"c_zh{q}", [128, NCH, T + 128], F32, dma=True) for q in range(2)])
            acc = k.sb(st, "c_acc", [128, NCH, T], F32, nbs=NCH)
            gbt = Rot([k.sb(st, f"c_gb{q}", [128, NCH, T], F32, dma=True) for q in range(1)])
            u = k.sb(st, "c_u", [128, NCH, T], BF16, nbs=NCH)
            tmp = Rot([k.sb(st, f"c_tmp{q}", [128, T], F32) for q in range(3)])
            ws = Rot([k.sb(st, f"c_w{q}", [128, 4096], BF16, dma=True) for q in range(4)])
            wo = k.sb(st, "c_wo", [128, NCH, NCH, 128], BF16, dma=True)
            ps = Rot([k.ps(st, f"c_ps{q}") for q in range(6)])
            psn = k.ps(st, "c_psn")
            k.dma(k.sp, wo.t[:], wos.rearrange("nb p k n -> p nb k n"), wo.ds, reads=[wob], writes=[wo.b])

            def proj_chunk(w, nn, Tn):
                p = ps.next()
                for kc in range(NCH):
                    k.op(k.pe, lambda e: e.matmul(p.t[:, :Tn], lhsT=w.t[:, kc * 512 + nn * 128:kc * 512 + (nn + 1) * 128], rhs=hT.t[:, kc, :Tn],
                                                  start=(kc == 0), stop=(kc == NCH - 1)),
                         reads=[w.b, hT.b], writes=[p.b] if kc == 0 else [], pw=[p.b] if kc else [], inc=(kc == NCH - 1))
                return p

            def loadw(nb):
                w = ws.next()
                k.dma(k.sp, w.t[:, :], wins[nb].rearrange("p k n -> p (k n)"), w.ds, reads=[winb], writes=[w.b])
                return w

            def compute_z(zt, zoff, Tn):
                for cbk in range(2):
                    wgc = loadw(2 + cbk)
                    wxi = loadw(4 + cbk)
                    for cc in range(4):
                        c = cbk * 4 + cc
                        p1 = proj_chunk(wgc, cc, Tn)
                        tp = tmp.next()
                        k.op(k.act, lambda e: e.activation(out=tp.t[:, :Tn], in_=p1.t[:, :Tn], func=AF.Copy), reads=[p1.b], writes=[tp.b])
                        p2 = proj_chunk(wxi, cc, Tn)
                        k.op(k.dve, lambda e: e.tensor_tensor(out=zt.t[:, c, zoff:zoff + Tn], in0=tp.t[:, :Tn], in1=p2.t[:, :Tn], op=ALU.mult),
                             reads=[tp.b, p2.b], pw=[zt.b])

            def out_stage(x, Tn, s, gb_from_psum):
                for cbk in range(2):
                    wgb = loadw(cbk) if gb_from_psum is None else None
                    for cc in range(4):
                        c = cbk * 4 + cc
                        if gb_from_psum is None:
                            p = proj_chunk(wgb, cc, Tn)
                            k.op(k.dve, lambda e: e.tensor_tensor(out=u.t[:, c, :Tn], in0=acc.t[:, c, :Tn], in1=p.t[:, :Tn], op=ALU.mult),
                                 reads=[acc.bs[c], p.b], writes=[u.bs[c]])
                        else:
                            g = gb_from_psum
                            k.op(k.pool, lambda e: e.tensor_tensor(out=u.t[:, c, :Tn], in0=acc.t[:, c, :Tn], in1=g.t[:, c, :Tn], op=ALU.mult),
                                 reads=[acc.bs[c], g.b], writes=[u.bs[c]])
                for c in range(NCH):
                    p = ps.next()
                    for kc in range(NCH):
                        k.op(k.pe, lambda e: e.matmul(p.t[:, :Tn], lhsT=wo.t[:, c, kc, :], rhs=u.t[:, kc, :Tn], start=(kc == 0), stop=(kc == NCH - 1)),
                             reads=[wo.b, u.bs[kc]], writes=[p.b] if kc == 0 else [], pw=[p.b] if kc else [], inc=(kc == NCH - 1))
                    k.op(k.dve, lambda e: e.scalar_tensor_tensor(out=x.t[:, c, :Tn], in0=p.t[:, :Tn], scalar=self.ada(i, 2, c, s), in1=x.t[:, c, :Tn],
                                                                 op0=ALU.mult, op1=ALU.add),
                         reads=[p.b, self.adaT.b, x.b], pw=[x.b])

            def taps(zt, zoff, Tn, mode):
                for c in range(NCH):
                    k.op(k.pool, lambda e: e.tensor_scalar(out=acc.t[:, c, :Tn], in0=zt.t[:, c, zoff:zoff + Tn], scalar1=cw(1, c), scalar2=cb(c),
                                                           op0=ALU.mult, op1=ALU.add),
                         reads=[zt.b, self.par.b], writes=[acc.bs[c]])
                    if mode == "rows":
                        a3 = acc.t[:, c, :Tn].rearrange("p (r m) -> p r m", m=64)
                        z3 = zt.t[:, c, zoff:zoff + Tn].rearrange("p (r m) -> p r m", m=64)
                        pairs = [(a3[:, :, 1:64], z3[:, :, 0:63], 0), (a3[:, :, 0:63], z3[:, :, 1:64], 2)]
                    elif mode == "seq":
                        pairs = [(acc.t[:, c, 1:Tn], zt.t[:, c, zoff:zoff + Tn - 1], 0), (acc.t[:, c, 0:Tn - 1], zt.t[:, c, zoff + 1:zoff + Tn], 2)]
                    else:
                        pairs = [(acc.t[:, c, :Tn], zt.t[:, c, zoff - 64:zoff - 64 + Tn], 0), (acc.t[:, c, :Tn], zt.t[:, c, zoff + 64:zoff + 64 + Tn], 2)]
                    for (a_ap, z_ap, tap) in pairs:
                        k.op(k.dve, lambda e: e.scalar_tensor_tensor(out=a_ap, in0=z_ap, scalar=cw(tap, c), in1=a_ap, op0=ALU.mult, op1=ALU.add),
                             reads=[zt.b, self.par.b, acc.bs[c]], writes=[acc.bs[c]])

            if axis2:
                jobs = ([self.job_ctx] if ctx_live else []) + self.jobs_lat
                for (c0, Tn, s) in jobs:
                    x = xt.next()
                    k.dma(k.sp, x.t[:, :, :Tn], self.xs_ap(self.xs, c0, Tn), x.ds, reads=[self.xsb[c0]], writes=[x.b])
                    self.norm_mod(x, Tn, hT, big, rt, rstd, psn, self.A1, i, 0, s)
                    zt = zh.next()
                    compute_z(zt, 0, Tn)
                    taps(zt, 0, Tn, "seq" if s == 1 else "rows")
                    out_stage(x, Tn, s, None)
                    k.dma(k.sp, self.xs_ap(self.xs, c0, Tn), x.t[:, :, :Tn], x.ds, reads=[x.b], writes=[self.xsb[c0]])
            else:
                zt_last = None
                for (c0, Tn, s) in self.jobs_lat:
                    x = xt.next()
                    k.dma(k.sp, x.t[:, :, :Tn], self.xs_ap(self.xs, c0, Tn), x.ds, reads=[self.xsb[c0]], writes=[x.b])
                    self.norm_mod(x, Tn, hT, big, rt, rstd, psn, self.A1, i, 0, s)
                    zt = zh.next()
                    compute_z(zt, 0, Tn)
                    k.dma(k.sp, self.xs_ap(self.zs, c0, Tn), zt.t[:, :, 0:Tn], zt.ds, reads=[zt.b], pw=[self.zsb])
                    g = gbt.next()
                    for cbk in range(2):
                        wgb = loadw(cbk)
                        for cc in range(4):
                            c = cbk * 4 + cc
                            p = proj_chunk(wgb, cc, Tn)
                            k.op(k.act, lambda e: e.activation(out=g.t[:, c, :Tn], in_=p.t[:, :Tn], func=AF.Copy), reads=[p.b], pw=[g.b])
                    k.dma(k.sp, self.xs_ap(self.gbs, c0, Tn), g.t[:, :, :Tn], g.ds, reads=[g.b], pw=[self.gbb])
                    zt_last = zt
                hb = k.sb(st, "c_hb", [128, 2, 512], F32, dma=True)
                hs = k.sb(st, "c_hs", [128, NCH, 64], F32, dma=True)
                hr = k.sb(st, "c_hr", [128, NCH, 64], F32, dma=True)
                cci = self.cch_in
                cco = self.cch_out
                k.dma(k.pool, cci.ap().rearrange("p (c m) -> p c m", m=64), zt_last.t[:, :, T - 64:T], hb.ds, reads=[zt_last.b], writes=[self.ccb_in])
                self.collective(cci, cco)
                k.dma(k.pool, hb.t[:], cco.ap().rearrange("(r p) n -> p r n", p=128), hb.ds, reads=[self.ccb_out], writes=[hb.b])
                self.select_partner(hb, hs.t[:].rearrange("p c m -> p (c m)"), hs.b, 512)
                k.op(k.dve, lambda e: e.tensor_copy(out=hr.t[:], in_=hs.t[:, :, ::-1]), reads=[hs.b], writes=[hr.b])
                k.dma(k.sp, self.xs_ap(self.zs, NL, 64), hr.t[:], hr.ds, reads=[hr.b], pw=[self.zsb])
                zbufs = zh.items
                for zb_ in zbufs:
                    k.op(k.pool, lambda e: e.memset(zb_.t[:, :, 0:64], 0.0), writes=[zb_.b])
                zr = Rot(zbufs)
                for (c0, Tn, s) in self.jobs_lat:
                    x = xt.next()
                    k.dma(k.sp, x.t[:, :, :Tn], self.xs_ap(self.xs, c0, Tn), x.ds, reads=[self.xsb[c0]], writes=[x.b])
                    zt = zr.next()
                    if c0 == 0:
                        k.dma(k.sp, zt.t[:, :, 64:Tn + 128], self.xs_ap(self.zs, 0, Tn + 64), zt.ds, reads=[self.zsb], pw=[zt.b])
                    else:
                        k.dma(k.sp, zt.t[:, :, 0:Tn + 128], self.xs_ap(self.zs, c0 - 64, Tn + 128), zt.ds, reads=[self.zsb], writes=[zt.b])
                    g = gbt.next()
                    k.dma(k.sp, g.t[:, :, :Tn], self.xs_ap(self.gbs, c0, Tn), g.ds, reads=[self.gbb], writes=[g.b])
                    taps(zt, 64, Tn, "halo")
                    out_stage(x, Tn, s, g)
                    k.dma(k.sp, self.xs_ap(self.xs, c0, Tn), x.t[:, :, :Tn], x.ds, reads=[x.b], writes=[self.xsb[c0]])

    def collective(self, cin, cout):
        k = self.k
        E = k.pool
        k._deps(E, [self.ccb_in], [self.ccb_out], [])
        E.eng.collective_compute("AllGather", ALU.bypass, replica_groups=GROUPS, ins=[cin.ap().opt()], outs=[cout.ap().opt()]).then_inc(self.ccsem.sem, 1)
        self.ccsem.count += 1
        tok = Tok(self.ccsem.sid, self.ccsem.sem, self.ccsem.count, None)
        k._upd(tok, [self.ccb_in], [self.ccb_out], [])

    def select_partner(self, hb, out_ap, out_buf, n):
        k = self.k
        k.op(k.dve, lambda e: e.tensor_scalar(out=out_ap, in0=hb.t[:, 0, :n], scalar1=self.sel[:, 0:1], scalar2=None, op0=ALU.mult),
             reads=[hb.b, self.cst.b], writes=[out_buf])
        k.op(k.dve, lambda e: e.scalar_tensor_tensor(out=out_ap, in0=hb.t[:, 1, :n], scalar=self.sel[:, 1:2], in1=out_ap, op0=ALU.mult, op1=ALU.add),
             reads=[hb.b, self.cst.b, out_buf], writes=[out_buf])


def _consts():
    c = np.zeros((128, 128 + 128 + 512 + 512 + 512 + 64 + 2), np.float32)
    c[:, 0:128] = np.eye(128, dtype=np.float32)
    c[:, 128:256] = 1.0
    sm = np.ones((128, 512), np.float32)
    sm[:, ::64] = 0.0
    c[:, 256:768] = sm
    s_idx = np.arange(128)[:, None]
    c_idx = np.arange(128)[None, :]
    same = (s_idx // 64) == (c_idx // 64)
    m1 = (same & (c_idx >= s_idx)).astype(np.float32)
    m2 = (same & (c_idx <= s_idx)).astype(np.float32)
    c[:, 768:1280] = np.tile(m1, (1, 4))
    c[:, 1280:1792] = np.tile(m2, (1, 4))
    c[:64, 1792:1856] = np.eye(64, dtype=np.float32)[::-1]
    return c


def _pp(v):
    v = np.asarray(v, np.float32)
    lead = v.shape[:-1]
    return np.ascontiguousarray(np.moveaxis(v.reshape(*lead, NCH, 128), -1, 0))


def make_in_maps(inputs, xstate=None):
    x, c, ctx, c_ctx = inputs["x"], inputs["c"], inputs["ctx"], inputs["c_ctx"]
    consts = _consts()
    maps = []
    for r in range(8):
        b, half = r // 2, r % 2
        m = {}
        if xstate is not None:
            m["xin"] = xstate[r]
        else:
            if half == 0:
                xl = x[b, :NL]
                cl = ctx[b]
            else:
                xl = x[b, NL:][::-1]
                cl = ctx[b][::-1]
            m["xin"] = np.ascontiguousarray(np.concatenate([xl, cl], axis=0).T)
        cv = np.stack([c[b], c_ctx], axis=-1)
        m["cvec"] = np.ascontiguousarray(cv.reshape(NCH, 128, 2).transpose(1, 0, 2))
        cs = consts.copy()
        cs[:, 1856 + (1 - half)] = 1.0
        m["consts"] = cs
        m["ada_w"] = inputs["ada_w"]
        m["adab"] = np.ascontiguousarray(inputs["ada_b"].reshape(4, 48, 128).transpose(2, 0, 1))
        m["nrm"] = np.ascontiguousarray(np.concatenate([_pp(inputs["norm1"]), _pp(inputs["norm2"]), _pp(inputs["norm_f"][None])], axis=1))
        m["mlp_w1"] = inputs["mlp_w1"]
        m["mlp_w2"] = inputs["mlp_w2"]
        hw = inputs["hgrn_w_in"]
        lb = inputs["hgrn_lb"]
        cw = inputs["conv_w"]
        if half == 1:
            hw = np.concatenate([hw[:, :, D:2 * D], hw[:, :, 0:D], hw[:, :, 2 * D:]], axis=2)
            lb = lb[::-1]
            cw = cw[:, ::-1]
        m["hgrn_w_in"] = np.ascontiguousarray(hw)
        m["lbv"] = _pp(lb)
        m["gnorm"] = _pp(inputs["hgrn_gnorm"])
        m["hgrn_w_out"] = inputs["hgrn_w_out"]
        m["conv_w_in"] = inputs["conv_w_in"]
        m["convw"] = _pp(cw)
        m["convb"] = _pp(inputs["conv_b"])
        m["conv_w_out"] = inputs["conv_w_out"]
        maps.append(m)
    return maps


def gather_out(res):
    out = np.empty((NB, SEQ, D), np.float32)
    for r in range(8):
        b, half = r // 2, r % 2
        y = res[r]["yout"].T
        if half == 0:
            out[b, :NL] = y
        else:
            out[b, NL:] = y[::-1]
    return out


def kernel(**inputs):
    inputs = {k: np.asarray(v) for k, v in inputs.items()}
    prog = Prog(ALL_STEPS, True)
    maps = make_in_maps(inputs)
    res = run_bass_kernel_spmd(prog.nc, maps, core_ids=list(range(8)))
    return gather_out(res.results)
```
